# Optimizing a Trainium2 kernel written in Bass

```python
import math
import jax, jax.numpy as jnp
from jax import lax
import numpy as np

D_MODEL = 1024
BATCH = 8
SEQ = 2048
DEPTH = 4
DEC_BATCH = 128
DEC_SEQ = 4
PAST_LEN = 16384
PAGE_SIZE = 128

N_MIXERS = 3
N_A = (DEPTH + 2) // 3
N_B = (DEPTH + 1) // 3
N_C = DEPTH // 3
EPS = 1e-6
CHUNK = 128
D_SGU = D_MODEL
HEADS_A = 8
HD_A = D_SGU // HEADS_A
D_RNN = D_MODEL
HEADS_B = 16
HD_B = D_RNN // HEADS_B
CONV_W = 4
LRU_C = 8.0
D_S5 = D_MODEL
GROUP_C = 16
G_C = D_S5 // GROUP_C
P_C = 64
D_FF = 4 * D_MODEL

kernel_name = "hybrid_sgu_rglru_s5_decoder_step"


def rmsnorm(x, g):
    xf = x.astype(jnp.float32)
    y = xf * lax.rsqrt(jnp.mean(xf * xf, axis=-1, keepdims=True) + EPS)
    return (y * g.astype(jnp.float32)).astype(x.dtype)


def chunk_sgu(h, w_in, sgu_g, w_s, b_s, w_out):
    bsz, T, _ = h.shape
    u, v = jnp.split(jax.nn.gelu(h @ w_in), 2, axis=-1)
    v = rmsnorm(v, sgu_g)
    L = min(T, CHUNK)
    nc = T // L
    mask = jnp.tril(jnp.ones((L, L), dtype=bool))
    w = jnp.where(mask, w_s[:, :L, :L], 0.0)
    vc = v.reshape(bsz, nc, L, HEADS_A, HD_A)
    mixed = jnp.einsum('gts,bcsgd->bctgd', w, vc) + b_s[:, :L].T[None, None, :, :, None]
    y = u * mixed.reshape(bsz, T, D_SGU)
    return y @ w_out, v


def rglru_block(h, conv_buf, h0, w_in, conv_w, conv_b, w_a, b_a, w_x, b_x, lam, w_out):
    bsz, T, _ = h.shape
    gate, xb = jnp.split(h @ w_in, 2, axis=-1)
    gate = jax.nn.gelu(gate)
    x_ext = jnp.concatenate([conv_buf.astype(xb.dtype), xb], axis=1)
    xc = conv_b + sum(x_ext[:, k:k + T] * conv_w[k] for k in range(CONV_W))
    new_buf = x_ext[:, T:]
    xh = xc.reshape(bsz, T, HEADS_B, HD_B)
    r = jax.nn.sigmoid(jnp.einsum('bthi,hij->bthj', xh, w_a).reshape(bsz, T, D_RNN) + b_a)
    i = jax.nn.sigmoid(jnp.einsum('bthi,hij->bthj', xh, w_x).reshape(bsz, T, D_RNN) + b_x)
    log_a = -LRU_C * r.astype(jnp.float32) * jax.nn.softplus(-lam.astype(jnp.float32))
    a = jnp.exp(log_a)
    mult = jnp.sqrt(jnp.maximum(-jnp.expm1(2.0 * log_a), 0.0))
    bx = mult * (i * xc).astype(jnp.float32)
    bx = bx.at[:, 0].add(a[:, 0] * h0.astype(jnp.float32))

    def comb(left, right):
        a1, b1 = left
        a2, b2 = right
        return a1 * a2, a2 * b1 + b2

    _, hs = lax.associative_scan(comb, (a, bx), axis=1)
    y = (hs.astype(h.dtype) * gate) @ w_out
    return y, new_buf, hs[:, -1].astype(h0.dtype)


def s5_block(h, s_re, s_im, w_in, lam_re, lam_im, log_dt, b_re, b_im, c_re, c_im, d_skip, w_glu):
    f32 = jnp.float32
    bsz, T, _ = h.shape
    u = h @ w_in
    dt = jnp.exp(log_dt.astype(f32))[:, None]
    lr, li = lam_re.astype(f32), lam_im.astype(f32)
    mag = jnp.exp(lr * dt)
    ab_re, ab_im = mag * jnp.cos(li * dt), mag * jnp.sin(li * dt)
    zr, zi = ab_re - 1.0, ab_im
    den = lr * lr + li * li
    q_re = (zr * lr + zi * li) / den
    q_im = (zi * lr - zr * li) / den
    br, bi = b_re.astype(f32), b_im.astype(f32)
    bb_re = q_re[..., None] * br - q_im[..., None] * bi
    bb_im = q_re[..., None] * bi + q_im[..., None] * br
    cr, ci = c_re.astype(f32), c_im.astype(f32)
    L = min(T, CHUNK)
    nc = T // L
    uc = u.astype(f32).reshape(bsz, nc, L, G_C, GROUP_C).transpose(1, 0, 2, 3, 4)

    def comb(left, right):
        ar1, ai1, br1, bi1 = left
        ar2, ai2, br2, bi2 = right
        return (ar1 * ar2 - ai1 * ai2, ar1 * ai2 + ai1 * ar2,
                ar2 * br1 - ai2 * bi1 + br2, ar2 * bi1 + ai2 * br1 + bi2)

    def step(carry, u_blk):
        x_re, x_im = carry
        bu_re = jnp.einsum('blgh,gph->blgp', u_blk, bb_re)
        bu_im = jnp.einsum('blgh,gph->blgp', u_blk, bb_im)
        bu_re = bu_re.at[:, 0].add(ab_re * x_re - ab_im * x_im)
        bu_im = bu_im.at[:, 0].add(ab_re * x_im + ab_im * x_re)
        a_re = jnp.broadcast_to(ab_re, bu_re.shape)
        a_im = jnp.broadcast_to(ab_im, bu_im.shape)
        _, _, xs_re, xs_im = lax.associative_scan(comb, (a_re, a_im, bu_re, bu_im), axis=1)
        y = jnp.einsum('blgp,ghp->blgh', xs_re, cr) - jnp.einsum('blgp,ghp->blgh', xs_im, ci)
        return (xs_re[:, -1], xs_im[:, -1]), y

    (fr, fi), ys = lax.scan(step, (s_re.astype(f32), s_im.astype(f32)), uc)
    y = ys.transpose(1, 0, 2, 3, 4).reshape(bsz, T, D_S5) + d_skip.astype(f32) * u.astype(f32)
    g = jax.nn.gelu(y).astype(h.dtype)
    o_a, o_b = jnp.split(g @ w_glu, 2, axis=-1)
    return o_a * jax.nn.sigmoid(o_b), fr.astype(s_re.dtype), fi.astype(s_im.dtype)


def sqrelu_ffn(h, w1, w2):
    return jnp.square(jax.nn.relu(h @ w1)) @ w2


def trunk(x, conv0, h0, sre0, sim0, p, keep_chunk_v):
    vs, convs, hs, sres, sims = [], [], [], [], []
    for layer in range(DEPTH):
        j = layer // N_MIXERS
        kind = layer % N_MIXERS
        h = rmsnorm(x, p['norm_mix'][layer])
        if kind == 0:
            out, v = chunk_sgu(h, p['w_in_a'][j], p['sgu_g'][j], p['w_s'][j], p['b_s'][j], p['w_out_a'][j])
            if keep_chunk_v:
                vs.append(v)
        elif kind == 1:
            out, cb, hl = rglru_block(h, conv0[j], h0[j], p['w_in_b'][j], p['conv_w'][j], p['conv_b'][j],
                                      p['w_a'][j], p['b_a'][j], p['w_x'][j], p['b_x'][j], p['lam'][j],
                                      p['w_out_b'][j])
            convs.append(cb)
            hs.append(hl)
        else:
            out, fr, fi = s5_block(h, sre0[j], sim0[j], p['w_in_c'][j], p['lam_re'][j], p['lam_im'][j],
                                   p['log_dt'][j], p['b_re'][j], p['b_im'][j], p['c_re'][j], p['c_im'][j],
                                   p['d_skip'][j], p['w_glu'][j])
            sres.append(fr)
            sims.append(fi)
        x = x + out.astype(x.dtype)
        x = x + sqrelu_ffn(rmsnorm(x, p['norm_ffn'][layer]), p['w_ff1'][layer], p['w_ff2'][layer]).astype(x.dtype)
    v_out = jnp.stack(vs) if keep_chunk_v else None
    return rmsnorm(x, p['norm_f']), v_out, jnp.stack(convs), jnp.stack(hs), jnp.stack(sres), jnp.stack(sims)


def setup_inputs(seed: int = 0) -> dict:
    key = jax.random.key(seed)
    ks = list(jax.random.split(key, 40))
    f32 = jnp.float32

    def nrm(idx, shape, scale):
        return jax.random.normal(ks[idx], shape, f32) * scale

    u_lam = jax.random.uniform(ks[20], (N_B, D_RNN), f32, minval=0.9, maxval=0.999)
    s_lam = u_lam ** (1.0 / LRU_C)
    lam = jnp.log(s_lam) - jnp.log1p(-s_lam)
    lam_im = jnp.broadcast_to(jnp.pi * jnp.arange(P_C, dtype=f32), (N_C, G_C, P_C)) + nrm(25, (N_C, G_C, P_C), 0.01)
    log_dt = jax.random.uniform(ks[26], (N_C, G_C), f32, minval=math.log(1e-3), maxval=math.log(1e-1))
    return {
        'x_prompt': nrm(0, (BATCH, SEQ, D_MODEL), 1.0),
        'x_sample': nrm(1, (DEC_BATCH, DEC_SEQ, D_MODEL), 1.0),
        'state_rglru_conv': nrm(2, (N_B, DEC_BATCH, CONV_W - 1, D_RNN), 1.0),
        'state_rglru_h': nrm(3, (N_B, DEC_BATCH, D_RNN), 0.5),
        'state_s5_re': nrm(4, (N_C, DEC_BATCH, G_C, P_C), 0.5),
        'state_s5_im': nrm(5, (N_C, DEC_BATCH, G_C, P_C), 0.5),
        'norm_mix': 1.0 + nrm(6, (DEPTH, D_MODEL), 0.01),
        'norm_ffn': 1.0 + nrm(7, (DEPTH, D_MODEL), 0.01),
        'norm_f': 1.0 + nrm(8, (D_MODEL,), 0.01),
        'w_ff1': nrm(9, (DEPTH, D_MODEL, D_FF), D_MODEL ** -0.5),
        'w_ff2': nrm(10, (DEPTH, D_FF, D_MODEL), D_FF ** -0.5),
        'w_in_a': nrm(11, (N_A, D_MODEL, 2 * D_SGU), D_MODEL ** -0.5),
        'sgu_g': 1.0 + nrm(12, (N_A, D_SGU), 0.01),
        'w_s': nrm(13, (N_A, HEADS_A, CHUNK, CHUNK), CHUNK ** -0.5),
        'b_s': 1.0 + nrm(14, (N_A, HEADS_A, CHUNK), 0.01),
        'w_out_a': nrm(15, (N_A, D_SGU, D_MODEL), D_SGU ** -0.5),
        'w_in_b': nrm(16, (N_B, D_MODEL, 2 * D_RNN), D_MODEL ** -0.5),
        'conv_w': nrm(17, (N_B, CONV_W, D_RNN), CONV_W ** -0.5),
        'conv_b': nrm(18, (N_B, D_RNN), 0.01),
        'w_a': nrm(19, (N_B, HEADS_B, HD_B, HD_B), HD_B ** -0.5),
        'b_a': nrm(21, (N_B, D_RNN), 0.01),
        'w_x': nrm(22, (N_B, HEADS_B, HD_B, HD_B), HD_B ** -0.5),
        'b_x': nrm(23, (N_B, D_RNN), 0.01),
        'lam': lam,
        'w_out_b': nrm(24, (N_B, D_RNN, D_MODEL), D_RNN ** -0.5),
        'w_in_c': nrm(27, (N_C, D_MODEL, D_S5), D_MODEL ** -0.5),
        'lam_re': -0.5 + nrm(28, (N_C, G_C, P_C), 0.01),
        'lam_im': lam_im,
        'log_dt': log_dt,
        'b_re': nrm(29, (N_C, G_C, P_C, GROUP_C), (2.0 * GROUP_C) ** -0.5),
        'b_im': nrm(30, (N_C, G_C, P_C, GROUP_C), (2.0 * GROUP_C) ** -0.5),
        'c_re': nrm(31, (N_C, G_C, GROUP_C, P_C), (2.0 * P_C) ** -0.5),
        'c_im': nrm(32, (N_C, G_C, GROUP_C, P_C), (2.0 * P_C) ** -0.5),
        'd_skip': nrm(33, (N_C, D_S5), 0.5),
        'w_glu': nrm(34, (N_C, D_S5, 2 * D_MODEL), D_S5 ** -0.5),
    }


def reference(x_prompt, x_sample, state_rglru_conv, state_rglru_h, state_s5_re, state_s5_im,
              norm_mix, norm_ffn, norm_f, w_ff1, w_ff2,
              w_in_a, sgu_g, w_s, b_s, w_out_a,
              w_in_b, conv_w, conv_b, w_a, b_a, w_x, b_x, lam, w_out_b,
              w_in_c, lam_re, lam_im, log_dt, b_re, b_im, c_re, c_im, d_skip, w_glu):
    p = dict(norm_mix=norm_mix, norm_ffn=norm_ffn, norm_f=norm_f, w_ff1=w_ff1, w_ff2=w_ff2,
             w_in_a=w_in_a, sgu_g=sgu_g, w_s=w_s, b_s=b_s, w_out_a=w_out_a,
             w_in_b=w_in_b, conv_w=conv_w, conv_b=conv_b, w_a=w_a, b_a=b_a, w_x=w_x, b_x=b_x,
             lam=lam, w_out_b=w_out_b,
             w_in_c=w_in_c, lam_re=lam_re, lam_im=lam_im, log_dt=log_dt, b_re=b_re, b_im=b_im,
             c_re=c_re, c_im=c_im, d_skip=d_skip, w_glu=w_glu)
    bp = x_prompt.shape[0]
    dt_s = state_rglru_h.dtype
    conv0_p = jnp.zeros((N_B, bp, CONV_W - 1, D_RNN), dt_s)
    h0_p = jnp.zeros((N_B, bp, D_RNN), dt_s)
    sre0_p = jnp.zeros((N_C, bp, G_C, P_C), state_s5_re.dtype)
    sim0_p = jnp.zeros((N_C, bp, G_C, P_C), state_s5_im.dtype)
    y_prompt, _, conv_p, h_p, sre_p, sim_p = trunk(x_prompt, conv0_p, h0_p, sre0_p, sim0_p, p, False)
    y_sample, v_s, conv_s, h_s, sre_s, sim_s = trunk(x_sample, state_rglru_conv, state_rglru_h,
                                                    state_s5_re, state_s5_im, p, True)
    return (y_prompt, y_sample, v_s, conv_p, h_p, conv_s, h_s, sre_p, sim_p, sre_s, sim_s)
```

```python
import os
import numpy as np
from contextlib import ExitStack
import concourse.bass as bass
import concourse.mybir as mybir
from concourse.bass_utils import run_bass_kernel_spmd

F32 = mybir.dt.float32
BF16 = mybir.dt.bfloat16
I32 = mybir.dt.int32
AF = mybir.ActivationFunctionType
ALU = mybir.AluOpType

NCORES = 8
D = 1024
NP = 2048
NSEQ = 16
NS = 64
NT = NP + NS
EPS = 1e-6
TILES = [(0, 512), (512, 512), (1024, 512), (1536, 512), (2048, 64)]


class Buf:
    __slots__ = ("name", "w", "r")

    def __init__(self, name):
        self.name = name
        self.w = None
        self.r = []


class Sched:
    NDMA = 16

    def __init__(self, nc, stack):
        self.nc = nc
        self.eng = {"pe": nc.tensor, "act": nc.scalar, "dve": nc.vector, "pool": nc.gpsimd, "sp": nc.sync}
        self.sem = {}
        self.cnt = {}
        for k in ["pe", "act", "dve", "pool"]:
            self.sem[k] = stack.enter_context(nc.semaphore("s_" + k))
            self.cnt[k] = 0
        self.dq = {}
        for q in ["sp", "pool"]:
            sems = []
            for i in range(self.NDMA):
                key = "d_%s_%d" % (q, i)
                self.sem[key] = stack.enter_context(nc.semaphore(key))
                self.cnt[key] = 0
                sems.append(key)
            self.dq[q] = [sems, 0]
        self.seen = {}
        self.nbuf = 0

    def buf(self, name=None):
        self.nbuf += 1
        return Buf(name or "b%d" % self.nbuf)

    def bufs(self, n):
        return [self.buf() for _ in range(n)]

    def _wait(self, e, needs):
        for key, val in needs.items():
            if val <= 0 or self.seen.get((e, key), 0) >= val:
                continue
            if e == "pe" and key == "pe":
                continue
            self.eng[e].wait_ge(self.sem[key], val)
            self.seen[(e, key)] = val

    def _needs(self, reads, writes):
        needs = {}

        def add(kv):
            if kv is not None and needs.get(kv[0], 0) < kv[1]:
                needs[kv[0]] = kv[1]
        for b in reads:
            add(b.w)
        for b in writes:
            add(b.w)
            for r in b.r:
                add(r)
        return needs

    def op(self, e, fn, reads=(), writes=(), inc=True):
        self._wait(e, self._needs(reads, writes))
        ins = fn()
        if inc:
            self.cnt[e] += 1
            ins.then_inc(self.sem[e], 1)
            tag = (e, self.cnt[e])
        else:
            tag = (e, self.cnt[e] + 1)
        for b in reads:
            b.r.append(tag)
            if len(b.r) > 64:
                b.r = self._compact(b.r)
        for b in writes:
            b.w = tag
            b.r = []
        return ins

    @staticmethod
    def _compact(lst):
        m = {}
        for k, v in lst:
            if m.get(k, 0) < v:
                m[k] = v
        return list(m.items())

    def dma(self, q, out, in_, reads=(), writes=(), **kw):
        sems, idx = self.dq[q]
        key = sems[idx % self.NDMA]
        self.dq[q][1] = idx + 1
        needs = self._needs(reads, writes)
        if self.cnt[key] > 0:
            needs[key] = max(needs.get(key, 0), self.cnt[key])
        self._wait(q, needs)
        ins = self.eng[q].dma_start(out=out, in_=in_, **kw)
        self.cnt[key] += 16
        ins.then_inc(self.sem[key], 16)
        tag = (key, self.cnt[key])
        for b in reads:
            b.r.append(tag)
        for b in writes:
            b.w = tag
            b.r = []
        return tag

    def finish(self, e="sp"):
        self._wait(e, {k: v for k, v in self.cnt.items() if v > 0})

    def barrier(self, pool=False):
        for e in ["pe", "act", "dve", "sp"] + (["pool"] if pool else []):
            self._wait(e, {k: v for k, v in self.cnt.items() if v > 0 and not k.startswith("d_pool")})


def build(depth=4):
    nc = bass.Bass("TRN2", target_bir_lowering=False)

    def din(name, shape):
        return nc.dram_tensor(name, list(shape), F32, kind="ExternalInput").ap()

    def dout(name, shape):
        return nc.dram_tensor(name, list(shape), F32, kind="ExternalOutput").ap()

    I = {}
    I["xp"] = din("xp", [NP, D])
    I["xs"] = din("xs", [NS, D])
    I["st_conv"] = din("st_conv", [NSEQ * 3, D])
    I["st_h"] = din("st_h", [NSEQ, D])
    I["st_re"] = din("st_re", [NSEQ, 4096])
    I["st_im"] = din("st_im", [NSEQ, 4096])
    for nm, shp in [("norm_mix", [4, D]), ("norm_ffn", [4, D]), ("norm_f", [D]),
                    ("w_ff1", [4, D, 4 * D]), ("w_ff2", [4, 4 * D, D]),
                    ("w_in_a", [2, D, 2 * D]), ("sgu_g", [2, D]), ("w_s", [2, 8, 128, 128]), ("b_s", [2, 8, 128]),
                    ("w_out_a", [2, D, D]), ("w_in_b", [1, D, 2 * D]), ("conv_w", [1, 4, D]), ("conv_b", [1, D]),
                    ("w_a", [1, 16, 64, 64]), ("b_a", [1, D]), ("w_x", [1, 16, 64, 64]), ("b_x", [1, D]),
                    ("lam", [1, D]), ("w_out_b", [1, D, D]), ("w_in_c", [1, D, D]),
                    ("lam_re", [1, 64, 64]), ("lam_im", [1, 64, 64]), ("log_dt", [1, 64]),
                    ("b_re", [1, 64, 64, 16]), ("b_im", [1, 64, 64, 16]),
                    ("c_re", [1, 64, 16, 64]), ("c_im", [1, 64, 16, 64]),
                    ("d_skip", [1, D]), ("w_glu", [1, D, 2 * D])]:
        if depth == 0 and nm.startswith("w_"):
            continue
        I[nm] = din(nm, shp)
    O = {}
    O["y_p"] = dout("y_p", [NP, D])
    O["y_s"] = dout("y_s", [NS, D])
    O["v_s"] = dout("v_s", [2, NS, D])
    O["conv_p"] = dout("conv_p", [3, D])
    O["h_p"] = dout("h_p", [D])
    O["conv_s"] = dout("conv_s", [NSEQ * 3, D])
    O["h_s"] = dout("h_s", [NSEQ, D])
    O["sre_p"] = dout("sre_p", [4096])
    O["sim_p"] = dout("sim_p", [4096])
    O["sre_s"] = dout("sre_s", [NSEQ, 4096])
    O["sim_s"] = dout("sim_s", [NSEQ, 4096])

    with ExitStack() as top:
        S = Sched(nc, top)

        tctr = [0]

        def T(st, name, shape, dt):
            tctr[0] += 1
            return st.enter_context(nc.sbuf_tensor("%s_%d" % (name, tctr[0]), list(shape), dt))

        x = T(top, "x", [128, 8, NT], F32)
        bx = S.bufs(5)
        W = T(top, "W", [128, 4, 8, 1024], BF16)
        bW = S.bufs(4)
        ps = top.enter_context(nc.psum_tensor("ps", [128, 8, 512], F32))
        psb = S.bufs(8)
        ident = T(top, "ident", [128, 128], F32); b_ident = S.buf()
        onesf = T(top, "onesf", [128, 128], F32); b_onesf = S.buf()
        onesb = T(top, "onesb", [128, 128], BF16); b_onesb = S.buf()
        tril = T(top, "tril", [128, 128], F32); b_tril = S.buf()
        cst = T(top, "cst", [128, 8], F32); b_cst = S.buf()
        gmix = T(top, "gmix", [128, 4, 8], F32)
        gffn = T(top, "gffn", [128, 4, 8], F32)
        gfin = T(top, "gfin", [128, 8], F32)
        b_g = S.buf()
        sq = T(top, "sq", [128, 8, 512], BF16); b_sq = S.buf()
        rs = T(top, "rs", [128, 512], F32); b_rs = S.buf()
        rs2 = T(top, "rs2", [128, 512], F32); b_rs2 = S.buf()

        bank_ctr = [0]

        def nb(n=1):
            if n == 2 and bank_ctr[0] % 2 == 1:
                bank_ctr[0] += 1
            b = bank_ctr[0] % 8
            bank_ctr[0] += n
            return b

        S.op("pool", lambda: nc.gpsimd.memset(onesf[:], 1.0), writes=[b_onesf])
        S.op("pool", lambda: nc.gpsimd.memset(onesb[:], 1.0), writes=[b_onesb])
        S.op("pool", lambda: nc.gpsimd.affine_select(ident[:], onesf[:], pattern=[[-1, 128]], compare_op=ALU.is_equal,
                                                      fill=0.0, base=0, channel_multiplier=1),
             reads=[b_onesf], writes=[b_ident])
        S.op("pool", lambda: nc.gpsimd.affine_select(tril[:], onesf[:], pattern=[[1, 128]], compare_op=ALU.is_ge,
                                                      fill=0.0, base=0, channel_multiplier=-1),
             reads=[b_onesf], writes=[b_tril])
        S.op("pool", lambda: nc.gpsimd.memset(cst[:, 0:1], EPS), writes=[b_cst])
        S.op("pool", lambda: nc.gpsimd.memset(cst[:, 1:2], 1.0), writes=[b_cst])
        S.op("pool", lambda: nc.gpsimd.memset(cst[:, 2:3], 0.0), writes=[b_cst])
        S.op("pool", lambda: nc.gpsimd.memset(cst[:, 3:4], -0.5), writes=[b_cst])
        S.op("pool", lambda: nc.gpsimd.memset(cst[:, 4:5], 0.5), writes=[b_cst])

        def small_dma(out, in_, writes, q="sp"):
            with nc.allow_non_contiguous_dma(reason="small param load"):
                return S.dma(q, out, in_, writes=writes)

        small_dma(gmix[:], I["norm_mix"].rearrange("l (c p) -> p l c", p=128), [b_g])
        small_dma(gffn[:], I["norm_ffn"].rearrange("l (c p) -> p l c", p=128), [b_g])
        small_dma(gfin[:], I["norm_f"].rearrange("(c p) -> p c", p=128), [b_g])
        s5p = {}
        if depth > 2:
            b_s5p = S.buf()
            for nm_ in ("lr", "li", "dtt"):
                s5p[nm_] = T(top, "s5p_" + nm_, [128, 4, 8], F32)
            s5p["dcol"] = T(top, "s5p_dcol", [128, 8], F32)

        wblocks = []
        for l in range(depth):
            kind = l % 3
            j = l // 3
            if kind == 0:
                wblocks += [I["w_in_a"][j, :, 0:1024], I["w_in_a"][j, :, 1024:2048], I["w_out_a"][j]]
            elif kind == 1:
                wblocks += [I["w_in_b"][j, :, 0:1024], I["w_in_b"][j, :, 1024:2048], I["w_out_b"][j]]
            else:
                wblocks += [I["w_in_c"][j], I["w_glu"][j, :, 0:1024], I["w_glu"][j, :, 1024:2048]]
            for q in range(4):
                wblocks += [I["w_ff1"][l, :, q * 1024:(q + 1) * 1024], I["w_ff2"][l, q * 1024:(q + 1) * 1024, :]]
        wnext = [0]

        def wissue(upto, after=()):
            lim = min(upto, len(wblocks) - 1)
            while wnext[0] <= lim:
                k = wnext[0]
                src = wblocks[k].rearrange("(kc p) f -> p kc f", p=128)
                for hh in range(2):
                    S.dma("pool", W[:, k % 4, 4 * hh:4 * hh + 4, :], src[:, 4 * hh:4 * hh + 4, :],
                          reads=list(after), writes=[bW[k % 4]])
                wnext[0] += 1

        def mm_acc(out_ap, pairs, reads, wbuf, **kw):
            n = len(pairs)
            for i, (l_, r_) in enumerate(pairs):
                S.op("pe", lambda: nc.tensor.matmul(out_ap, lhsT=l_, rhs=r_, start=(i == 0), stop=(i == n - 1), **kw),
                     reads=reads, writes=[wbuf], inc=(i == n - 1))

        def act(out, in_, func, reads, writes, **kw):
            return S.op("act", lambda: nc.scalar.activation(out=out, in_=in_, func=func, **kw), reads=reads, writes=writes)

        def tt(out, in0, in1, op, reads, writes, e="dve"):
            return S.op(e, lambda: S.eng[e].tensor_tensor(out, in0, in1, op), reads=reads, writes=writes)

        def ts(out, in0, s1, s2, op0, op1, reads, writes, e="dve"):
            if op1 is None:
                return S.op(e, lambda: S.eng[e].tensor_scalar(out, in0, s1, None, op0), reads=reads, writes=writes)
            return S.op(e, lambda: S.eng[e].tensor_scalar(out, in0, s1, s2, op0, op1), reads=reads, writes=writes)

        def stt(out, in0, scalar, in1, op0, op1, reads, writes):
            return S.op("dve", lambda: nc.vector.scalar_tensor_tensor(out, in0, scalar, in1, op0, op1),
                        reads=reads, writes=writes)

        def cp(out, in_, reads, writes, e="dve"):
            return S.op(e, lambda: S.eng[e].tensor_copy(out, in_), reads=reads, writes=writes)

        def transpose_to(bank, col0, in_ap, rows, cols, reads, inc=True):
            S.op("pe", lambda: nc.tensor.transpose(ps[:cols, bank, col0:col0 + rows], in_ap, ident[:rows, :rows]),
                 reads=list(reads) + [b_ident], writes=[psb[bank]], inc=inc)

        def norm_a(t):
            t0, n = TILES[t]
            act(sq[:, :, :n], x[:, :, t0:t0 + n], AF.Square, [bx[t]], [b_sq])

        def norm(t, g_ap, out_ap, out_buf, skip_a=False):
            t0, n = TILES[t]
            if not skip_a:
                norm_a(t)
            bk = nb()
            mm_acc(ps[:, bk, :n], [(onesb[:, :], sq[:, c, :n]) for c in range(8)], [b_sq, b_onesb], psb[bk])
            act(rs[:, :n], ps[:, bk, :n], AF.Sqrt, [psb[bk], b_cst], [b_rs], scale=1.0 / D, bias=cst[:, 0:1])
            S.op("dve", lambda: nc.vector.reciprocal(rs2[:, :n], rs[:, :n]), reads=[b_rs], writes=[b_rs2])
            for c in range(8):
                stt(out_ap[:, c, :n], x[:, c, t0:t0 + n], g_ap[:, c:c + 1], rs2[:, :n], ALU.mult, ALU.mult,
                    [bx[t], b_rs2, b_g], [out_buf])

        def resid_add(t, oc, bk):
            t0, n = TILES[t]
            tt(x[:, oc, t0:t0 + n], ps[:, bk, :n], x[:, oc, t0:t0 + n], ALU.add, [psb[bk], bx[t]], [bx[t]])

        def out_proj(t, Wo, bWo, ybuf, b_y):
            t0, n = TILES[t]
            for oc in range(8):
                bk = nb()
                mm_acc(ps[:, bk, :n], [(Wo[:, fc, oc * 128:(oc + 1) * 128], ybuf[:, fc, :n]) for fc in range(8)],
                       [bWo, b_y], psb[bk])
                resid_add(t, oc, bk)

        with ExitStack() as st:
            xin = [T(st, "xin%d" % i, [128, 1024], F32) for i in range(3)]
            b_xin = S.bufs(3)
            wst = T(st, "wst", [128, 8, 1024], F32); b_wst = S.buf()

            def fast_block(k):
                src = wblocks[k].rearrange("(kc p) f -> p kc f", p=128)
                for hh in range(2):
                    S.dma("sp", wst[:, 4 * hh:4 * hh + 4, :], src[:, 4 * hh:4 * hh + 4, :], writes=[b_wst])
                for hh in range(2):
                    cp(W[:, k % 4, 4 * hh:4 * hh + 4, :], wst[:, 4 * hh:4 * hh + 4, :], [b_wst], [bW[k % 4]])
                wnext[0] = max(wnext[0], k + 1)
            if depth > 0:
                fast_block(0)
            for blk in range(17):
                rows = 128 if blk < 16 else 64
                src = I["xp"][blk * 128:(blk + 1) * 128, :] if blk < 16 else I["xs"][:, :]
                t = blk // 4 if blk < 16 else 4
                tok0 = blk * 128
                xi, bxi = xin[blk % 3], b_xin[blk % 3]
                S.dma("sp", xi[:rows, :], src, writes=[bxi])
                for hh in range(2):
                    bk = nb()
                    for j_ in range(4):
                        c = 4 * hh + j_
                        transpose_to(bk, j_ * 128, xi[:rows, c * 128:(c + 1) * 128], rows, 128, [bxi], inc=(j_ == 3))
                    pv_ = ps[:, bk, :].rearrange("p (j t) -> p j t", j=4)[:, :, :rows]
                    if hh == 0:
                        act(x[:, 4 * hh:4 * hh + 4, tok0:tok0 + rows], pv_, AF.Copy, [psb[bk]], [bx[t]])
                    else:
                        cp(x[:, 4 * hh:4 * hh + 4, tok0:tok0 + rows], pv_, [psb[bk]], [bx[t]])
            if depth > 0:
                fast_block(1)
            S.barrier()

        def s5_param_prefetch():
            if depth <= 2:
                return
            pv_ = "(c kk g2) p -> (g2 p) kk c"
            for kk in range(4):
                small_dma(s5p["lr"][:, kk, :], I["lam_re"][0].rearrange(pv_, kk=4, g2=2)[:, kk, :], [b_s5p])
                small_dma(s5p["li"][:, kk, :], I["lam_im"][0].rearrange(pv_, kk=4, g2=2)[:, kk, :], [b_s5p])
                for g2 in range(2):
                    small_dma(s5p["dtt"][64 * g2:64 * g2 + 64, kk, :],
                              I["log_dt"][0].rearrange("(c kk g2) -> g2 kk c", kk=4, g2=2)[g2, kk].partition_broadcast(64), [b_s5p])
            small_dma(s5p["dcol"][:], I["d_skip"][0].rearrange("(c p) -> p c", p=128), [b_s5p])


        def store_rows(dst, tile_ap, reads):
            S.dma("sp", dst, tile_ap, reads=reads)

        def ffn(l, wb0):
            with ExitStack() as st:
                xn = T(st, "xn_all", [128, 8, NT], BF16)
                b_xn = S.bufs(5)
                hb = [T(st, "hb%d" % i, [128, 8, 512], BF16) for i in range(2)]
                b_hb = S.bufs(2)
                rl = [T(st, "rl%d" % i, [128, 512], F32) for i in range(2)]
                b_rl = S.bufs(2)
                norm(0, gffn[:, l, :], xn[:, :, TILES[0][0]:TILES[0][0] + TILES[0][1]], b_xn[0])
                rlc = [0]
                for q in range(4):
                    i1 = wb0 + 2 * q
                    i2 = i1 + 1
                    wissue(i2 + 2)
                    W1, W2 = W[:, i1 % 4], W[:, i2 % 4]
                    bW1, bW2 = bW[i1 % 4], bW[i2 % 4]

                    def ff1(t):
                        t0, n = TILES[t]
                        h_, bh_ = hb[t % 2], b_hb[t % 2]
                        if q == 0 and t + 1 < 5:
                            norm_a(t + 1)
                        for fc in range(8):
                            bk = nb()
                            mm_acc(ps[:, bk, :n],
                                   [(W1[:, kc, fc * 128:(fc + 1) * 128], xn[:, kc, t0:t0 + n]) for kc in range(8)],
                                   [bW1, b_xn[t]], psb[bk])
                            ri = rlc[0] % 2
                            rlc[0] += 1
                            act(rl[ri][:, :n], ps[:, bk, :n], AF.Relu, [psb[bk]], [b_rl[ri]])
                            act(h_[:, fc, :n], rl[ri][:, :n], AF.Square, [b_rl[ri]], [bh_])
                        if q == 0 and t + 1 < 5:
                            t1_, n1_ = TILES[t + 1]
                            norm(t + 1, gffn[:, l, :], xn[:, :, t1_:t1_ + n1_], b_xn[t + 1], skip_a=True)

                    def ff2(t):
                        t0, n = TILES[t]
                        h_, bh_ = hb[t % 2], b_hb[t % 2]
                        for oc in range(8):
                            bk = nb()
                            mm_acc(ps[:, bk, :n],
                                   [(W2[:, fc, oc * 128:(oc + 1) * 128], h_[:, fc, :n]) for fc in range(8)],
                                   [bW2, bh_], psb[bk])
                            resid_add(t, oc, bk)

                    ff1(0)
                    for t in range(5):
                        if t + 1 < 5:
                            ff1(t + 1)
                        ff2(t)
                S.barrier()

        def sgu(l, j, wb0):
            with ExitStack() as st:
                sggB = T(st, "sggB", [128, 1024], F32)
                bsB = T(st, "bsB", [128, 8, 128], F32)
                wsn = T(st, "wsn", [128, 8, 128], F32)
                WmT = T(st, "WmT", [128, 8, 128], BF16)
                wS32 = T(st, "wS32", [64, 8, 64], F32)
                WmS = T(st, "WmS", [64, 8, 64], BF16)
                b_c = S.buf(); b_wsn = S.buf(); b_WmT = S.buf(); b_wS = S.buf(); b_WmS = S.buf()
                xnT = [T(st, "xnT0", [128, 8, 512], BF16)] * 2
                b_xnT = [S.buf()] * 2
                u = T(st, "u", [128, 8, 512], F32); b_u = S.buf()
                vg = [T(st, "vg0", [128, 1024], F32)] * 2; b_vg = [S.buf()] * 2
                vb = [T(st, "vb%d" % i, [128, 1024], BF16) for i in range(2)]; b_vb = S.bufs(2)
                ss = T(st, "ss", [128, 4], F32); b_ss = S.buf()
                ssr = [T(st, "ssr%d" % i, [128, 1], F32) for i in range(2)]; b_ssr = S.bufs(2)
                tmp = T(st, "tmp", [128, 8, 128], F32); b_tmp = S.buf()
                ybuf = [T(st, "ybuf0", [128, 8, 512], BF16)] * 2; b_y = [S.buf()] * 2
                if l != 0:
                    wissue(wb0 + 3)
                Wu, Wv, Wo = W[:, wb0 % 4], W[:, (wb0 + 1) % 4], W[:, (wb0 + 2) % 4]
                bWu, bWv, bWo = bW[wb0 % 4], bW[(wb0 + 1) % 4], bW[(wb0 + 2) % 4]
                norm(0, gmix[:, l, :], xnT[0], b_xnT[0])
                for fc in range(8):
                    bk = nb()
                    mm_acc(ps[:, bk, :512], [(Wu[:, kc, fc * 128:(fc + 1) * 128], xnT[0][:, kc, :512]) for kc in range(8)],
                           [bWu, b_xnT[0]], psb[bk])
                    act(u[:, fc, :512], ps[:, bk, :512], AF.Gelu, [psb[bk]], [b_u])
                small_dma(sggB[:], I["sgu_g"][j].partition_broadcast(128), [b_c])
                small_dma(bsB[:].rearrange("p g t -> p (g t)"),
                          I["b_s"][j].rearrange("g t -> (g t)").partition_broadcast(128), [b_c])
                S.dma("sp", wsn[:], I["w_s"][j].rearrange("g t s -> t g s"), writes=[b_wsn])
                for hh in range(2):
                    bk = nb()
                    for j_ in range(4):
                        transpose_to(bk, j_ * 128, wsn[:, 4 * hh + j_, :], 128, 128, [b_wsn], inc=(j_ == 3))
                    tt(WmT[:, 4 * hh:4 * hh + 4, :], ps[:, bk, :].rearrange("p (j t) -> p j t", j=4),
                       tril[:, :].unsqueeze(1).to_broadcast([128, 4, 128]), ALU.mult, [psb[bk], b_tril], [b_WmT])
                S.op("dve", lambda: nc.vector.memset(wS32[:], 0.0), writes=[b_wS])
                w4t = T(st, "w4t", [4, 32], F32); b_w4 = S.buf()
                w4 = w4t[:, :].rearrange("p (g t) -> p g t", g=8)
                for g_ in range(8):
                    small_dma(w4[:, g_, :], I["w_s"][j, g_, 0:4, 0:4].rearrange("t s -> s t"), [b_w4])
                for b_ in range(NSEQ):
                    S.dma("sp", wS32[4 * b_:4 * b_ + 4, :, 4 * b_:4 * b_ + 4], w4, reads=[b_w4], writes=[b_wS])
                tt(WmS[:], wS32[:], tril[:64, :64].unsqueeze(1).to_broadcast([64, 8, 64]), ALU.mult,
                   [b_wS, b_tril], [b_WmS])
                if l == 0:
                    s5_param_prefetch()

                Wu, Wv, Wo = W[:, wb0 % 4], W[:, (wb0 + 1) % 4], W[:, (wb0 + 2) % 4]
                bWu, bWv, bWo = bW[wb0 % 4], bW[(wb0 + 1) % 4], bW[(wb0 + 2) % 4]
                vg[1] = wsn[:].rearrange("p g t -> p (g t)")
                b_vg[1] = b_wsn
                chc = [0]

                def stageA(t, ch, k_):
                    t0, n = TILES[t]
                    xn_, bxn_ = xnT[0], b_xnT[0]
                    cn = min(n, 128)
                    c0 = ch * 128
                    for hf in range(2):
                        bk = nb()
                        mm_acc(ps[:cn, bk, :],
                               [(xn_[:, kc, c0:c0 + cn], Wv[:, kc, hf * 512:(hf + 1) * 512]) for kc in range(8)],
                               [bWv, bxn_], psb[bk])
                        act(vg[k_][:cn, hf * 512:(hf + 1) * 512], ps[:cn, bk, :], AF.Gelu, [psb[bk]], [b_vg[k_]])
                    tmpf = tmp[:].rearrange("p g t -> p (g t)")
                    act(tmpf[:cn, :], vg[k_][:cn, :], AF.Square, [b_vg[k_]], [b_tmp])
                    S.op("dve", lambda: nc.vector.reduce_sum(ss[:cn, 0:1], tmpf[:cn, :], mybir.AxisListType.X),
                         reads=[b_tmp], writes=[b_ss])
                    ts(ss[:cn, 1:2], ss[:cn, 0:1], 1.0 / D, EPS, ALU.mult, ALU.add, [b_ss], [b_ss])
                    tt(ssr[k_][:cn, 0:1], ss[:cn, 1:2], cst[:cn, 3:4], ALU.pow, [b_ss, b_cst], [b_ssr[k_]], e="pool")

                def stageA2(t, ch, k_):
                    t0, n = TILES[t]
                    cn = min(n, 128)
                    stt(vb[k_][:cn, :], vg[k_][:cn, :], ssr[k_][:cn, 0:1], sggB[:cn, :], ALU.mult, ALU.mult,
                        [b_vg[k_], b_ssr[k_], b_c], [b_vb[k_]])
                    if t == 4:
                        vf_, bvf_ = vg[1 - k_], b_vg[1 - k_]
                        stt(vf_[:cn, :], vg[k_][:cn, :], ssr[k_][:cn, 0:1], sggB[:cn, :], ALU.mult, ALU.mult,
                            [b_vg[k_], b_ssr[k_], b_c], [bvf_])
                        S.dma("sp", O["v_s"][j], vf_[:cn, :], reads=[bvf_])

                def stageB(t, ch, k_):
                    t0, n = TILES[t]
                    yb_, by_ = ybuf[0], b_y[0]
                    cn = min(n, 128)
                    c0 = ch * 128
                    bk = nb(2)
                    pm = ps[:, bk:bk + 2, :].rearrange("p b f -> p (b f)")
                    for g in range(8):
                        rhs = WmT[:cn, g, :cn] if t < 4 else WmS[:cn, g, :cn]
                        bkk = bk + (g * cn) // 512
                        S.op("pe", lambda: nc.tensor.matmul(pm[:, g * cn:(g + 1) * cn],
                                                            lhsT=vb[k_][:cn, g * 128:(g + 1) * 128], rhs=rhs,
                                                            start=True, stop=True),
                             reads=[b_vb[k_], b_WmT, b_WmS], writes=[psb[bkk]],
                             inc=(g == 7 or (cn == 128 and g == 3)))
                    pmv = pm[:, :8 * cn].rearrange("p (g t) -> p g t", g=8)
                    rb = [psb[bk]] + ([psb[bk + 1]] if cn == 128 else [])
                    if t < 4:
                        tt(tmp[:, :, :cn], pmv, bsB[:, :, :cn], ALU.add, rb + [b_c], [b_tmp])
                    else:
                        tt(tmp[:, :, :cn].rearrange("p g (b t) -> p g b t", t=4),
                           pmv.rearrange("p g (b t) -> p g b t", t=4),
                           bsB[:, :, 0:4].unsqueeze(2).to_broadcast([128, 8, NSEQ, 4]), ALU.add, rb + [b_c], [b_tmp])
                    tt(yb_[:, :, c0:c0 + cn], tmp[:, :, :cn], u[:, :, c0:c0 + cn], ALU.mult, [b_tmp, b_u], [by_])

                def tile_norm(t):
                    norm(t, gmix[:, l, :], xnT[0], b_xnT[0])

                def tile_u(t):
                    t0, n = TILES[t]
                    for fc in range(8):
                        bk = nb()
                        mm_acc(ps[:, bk, :n], [(Wu[:, kc, fc * 128:(fc + 1) * 128], xnT[0][:, kc, :n]) for kc in range(8)],
                               [bWu, b_xnT[0]], psb[bk])
                        act(u[:, fc, :n], ps[:, bk, :n], AF.Gelu, [psb[bk]], [b_u])

                wissue(wb0 + 3)
                for t in range(5):
                    t0, n = TILES[t]
                    nch = max(1, n // 128)
                    ks = []
                    for ch in range(nch):
                        ks.append(chc[0] % 2)
                        chc[0] += 1
                    stageA(t, 0, ks[0])
                    stageA2(t, 0, ks[0])
                    for ch in range(nch):
                        if ch + 1 < nch:
                            stageA(t, ch + 1, ks[ch + 1])
                        elif t + 1 < 5:
                            tile_norm(t + 1)
                        stageB(t, ch, ks[ch])
                        if ch + 1 < nch:
                            stageA2(t, ch + 1, ks[ch + 1])
                    if t + 1 < 5:
                        tile_u(t + 1)
                    out_proj(t, Wo, bWo, ybuf[0], b_y[0])
                wissue(wb0 + 6)
                S.barrier(pool=True)

        def rglru(l, j, wb0):
            with ExitStack() as st:
                convw = T(st, "convw", [128, 8, 4], F32)
                pc = T(st, "pc", [128, 8, 8], F32)
                WaBD = T(st, "WaBD", [128, 8, 128], BF16)
                WxBD = T(st, "WxBD", [128, 8, 128], BF16)
                b_c = S.buf(); b_wa = S.buf()
                halo = T(st, "halo", [128, 8, 3], F32); b_halo = S.buf()
                hcar = T(st, "hcar", [128, 8], F32); b_hcar = S.buf()
                halo_s = T(st, "halo_s", [128, 8, NSEQ * 3], F32); b_halo_s = S.buf()
                h0s = T(st, "h0s", [128, 8, NSEQ], F32); b_h0s = S.buf()
                convs_o = T(st, "convs_o", [128, 8, NSEQ * 3], F32); b_convs_o = S.buf()
                hs_o = T(st, "hs_o", [128, 8, NSEQ], F32); b_hs_o = S.buf()
                stin = T(st, "stin", [64, 1024], F32); b_stin = S.buf()
                stout = stin; b_stout = b_stin
                xnT = [T(st, "xnTb0", [128, 8, 512], BF16)] * 2; b_xnT = [S.buf()] * 2
                ybuf = [T(st, "ybufb0", [128, 8, 512], BF16)] * 2; b_y = [S.buf()] * 2
                NSET = 2
                names = ["gate", "xc", "r", "ig", "a", "a2"]
                sets = []
                for i in range(NSET):
                    d_ = {nm: T(st, "%s%d" % (nm, i), [128, 512], F32) for nm in names}
                    d_["xext"] = T(st, "xext%d" % i, [128, 520], F32)
                    d_["xcb"] = T(st, "xcb%d" % i, [128, 512], BF16)
                    d_["b"] = {nm: S.buf() for nm in names + ["xext", "xcb"]}
                    d_["hc"] = d_["r"]
                    d_["b"]["hc"] = d_["b"]["r"]
                    sets.append(d_)
                wissue(wb0 + 3)
                for k in range(4):
                    small_dma(convw[:, :, k], I["conv_w"][j, k].rearrange("(c p) -> p c", p=128), [b_c])
                for idx, nm in enumerate(["conv_b", "b_a", "b_x", "lam"]):
                    small_dma(pc[:, :, idx], I[nm][j].rearrange("(c p) -> p c", p=128), [b_c])
                act(pc[:, :, 6], pc[:, :, 3], AF.Exp, [b_c], [b_c], scale=-1.0)
                act(pc[:, :, 4], pc[:, :, 6], AF.Ln, [b_c, b_cst], [b_c], bias=cst[:, 1:2])
                ts(pc[:, :, 5], pc[:, :, 4], -4.0, None, ALU.mult, None, [b_c], [b_c])
                ts(pc[:, :, 4], pc[:, :, 4], -8.0, None, ALU.mult, None, [b_c], [b_c])
                ts(pc[:, :, 1], pc[:, :, 1], 0.5, None, ALU.mult, None, [b_c], [b_c])
                ts(pc[:, :, 2], pc[:, :, 2], 0.5, None, ALU.mult, None, [b_c], [b_c])
                S.op("dve", lambda: nc.vector.memset(WaBD[:], 0.0), writes=[b_wa])
                S.op("dve", lambda: nc.vector.memset(WxBD[:], 0.0), writes=[b_wa])
                for (wt, nm) in [(WaBD, "w_a"), (WxBD, "w_x")]:
                    srcv = I[nm][j].rearrange("(c h2) i j -> h2 i c j", h2=2)
                    for h2 in range(2):
                        with nc.allow_non_contiguous_dma(reason="blockdiag"):
                            S.dma("pool", wt[64 * h2:64 * h2 + 64, :, 64 * h2:64 * h2 + 64], srcv[h2], writes=[b_wa])
                S.op("dve", lambda: nc.vector.memset(halo[:], 0.0), writes=[b_halo])
                S.op("dve", lambda: nc.vector.memset(hcar[:], 0.0), writes=[b_hcar])
                S.dma("sp", stin[:48, :], I["st_conv"][:, :], writes=[b_stin])
                for hh in range(2):
                    bk = nb()
                    for j_ in range(4):
                        c = 4 * hh + j_
                        transpose_to(bk, j_ * 128, stin[:48, c * 128:(c + 1) * 128], 48, 128, [b_stin], inc=(j_ == 3))
                    cp(halo_s[:, 4 * hh:4 * hh + 4, :], ps[:, bk, :].rearrange("p (j t) -> p j t", j=4)[:, :, :48],
                       [psb[bk]], [b_halo_s])
                S.dma("sp", stin[:16, :], I["st_h"][:, :], writes=[b_stin])
                for hh in range(2):
                    bk = nb()
                    for j_ in range(4):
                        c = 4 * hh + j_
                        transpose_to(bk, j_ * 128, stin[:16, c * 128:(c + 1) * 128], 16, 128, [b_stin], inc=(j_ == 3))
                    cp(h0s[:, 4 * hh:4 * hh + 4, :], ps[:, bk, :].rearrange("p (j t) -> p j t", j=4)[:, :, :16],
                       [psb[bk]], [b_h0s])

                Wg, Wx_, Wo = W[:, wb0 % 4], W[:, (wb0 + 1) % 4], W[:, (wb0 + 2) % 4]
                bWg, bWx, bWo = bW[wb0 % 4], bW[(wb0 + 1) % 4], bW[(wb0 + 2) % 4]
                def P1(t, c, Z):
                    t0, n = TILES[t]
                    smp = (t == 4)
                    xn_, bxn_ = xnT[0], b_xnT[0]
                    B = Z["b"]
                    bk = nb()
                    mm_acc(ps[:, bk, :n], [(Wg[:, kc, c * 128:(c + 1) * 128], xn_[:, kc, :n]) for kc in range(8)],
                           [bWg, bxn_], psb[bk])
                    act(Z["gate"][:, :n], ps[:, bk, :n], AF.Gelu, [psb[bk]], [B["gate"]])
                    bk = nb()
                    mm_acc(ps[:, bk, :n], [(Wx_[:, kc, c * 128:(c + 1) * 128], xn_[:, kc, :n]) for kc in range(8)],
                           [bWx, bxn_], psb[bk])
                    if not smp:
                        cp(Z["xext"][:, 3:3 + n], ps[:, bk, :n], [psb[bk]], [B["xext"]])
                        cp(Z["xext"][:, 0:3], halo[:, c, :], [b_halo], [B["xext"]])
                        cp(halo[:, c, :], Z["xext"][:, n:n + 3], [B["xext"]], [b_halo])

                        def xk(k):
                            return Z["xext"][:, k:k + n]
                        xcv = Z["xc"][:, :n]
                    else:
                        xe3 = Z["xext"][:, 0:NSEQ * 7].rearrange("p (b k) -> p b k", k=7)
                        cp(xe3[:, :, 3:7], ps[:, bk, :n].rearrange("p (b k) -> p b k", k=4),
                           [psb[bk]], [B["xext"]])
                        cp(xe3[:, :, 0:3], halo_s[:, c, :].rearrange("p (b k) -> p b k", k=3), [b_halo_s], [B["xext"]])
                        cp(convs_o[:, c, :].rearrange("p (b k) -> p b k", k=3), xe3[:, :, 4:7], [B["xext"]], [b_convs_o])

                        def xk(k):
                            return xe3[:, :, k:k + 4]
                        xcv = Z["xc"][:, :n].rearrange("p (b k) -> p b k", k=4)
                    ts(xcv, xk(0), convw[:, c, 0:1], pc[:, c, 0:1], ALU.mult, ALU.add, [B["xext"], b_c], [B["xc"]])
                    for k in range(1, 4):
                        stt(xcv, xk(k), convw[:, c, k:k + 1], xcv, ALU.mult, ALU.add, [B["xext"], b_c, B["xc"]], [B["xc"]])
                    cp(Z["xcb"][:, :n], Z["xc"][:, :n], [B["xc"]], [B["xcb"]])

                def P2(t, c, Z):
                    t0, n = TILES[t]
                    smp = (t == 4)
                    B = Z["b"]
                    yb_, by_ = ybuf[0], b_y[0]
                    bk = nb()
                    mm_acc(ps[:, bk, :n], [(WaBD[:, c, :], Z["xcb"][:, :n])], [b_wa, B["xcb"]], psb[bk])
                    act(Z["r"][:, :n], ps[:, bk, :n], AF.Tanh, [psb[bk], b_c], [B["r"]], scale=0.5, bias=pc[:, c, 1:2])
                    bk = nb()
                    mm_acc(ps[:, bk, :n], [(WxBD[:, c, :], Z["xcb"][:, :n])], [b_wa, B["xcb"]], psb[bk])
                    act(Z["ig"][:, :n], ps[:, bk, :n], AF.Tanh, [psb[bk], b_c], [B["ig"]], scale=0.5, bias=pc[:, c, 2:3])
                    act(Z["a"][:, :n], Z["r"][:, :n], AF.Exp, [B["r"], b_c], [B["a"]], scale=pc[:, c, 5:6], bias=pc[:, c, 5:6])
                    act(Z["a2"][:, :n], Z["r"][:, :n], AF.Exp, [B["r"], b_c], [B["a2"]], scale=pc[:, c, 4:5], bias=pc[:, c, 4:5])
                    act(Z["a2"][:, :n], Z["a2"][:, :n], AF.Sqrt, [B["a2"], b_cst], [B["a2"]], scale=-1.0, bias=cst[:, 1:2])
                    stt(Z["ig"][:, :n], Z["ig"][:, :n], 1.0, Z["xc"][:, :n], ALU.add, ALU.mult, [B["ig"], B["xc"]], [B["ig"]])
                    stt(Z["ig"][:, :n], Z["ig"][:, :n], 0.5, Z["a2"][:, :n], ALU.mult, ALU.mult, [B["ig"], B["a2"]], [B["ig"]])
                    if not smp:
                        S.op("dve", lambda: nc.vector.tensor_tensor_scan(Z["hc"][:, :n], Z["a"][:, :n], Z["ig"][:, :n],
                                                                         hcar[:, c:c + 1], ALU.mult, ALU.add),
                             reads=[B["a"], B["ig"], b_hcar], writes=[B["hc"]])
                        cp(hcar[:, c:c + 1], Z["hc"][:, n - 1:n], [B["hc"]], [b_hcar])
                    else:
                        a3 = Z["a"][:, :n].rearrange("p (b k) -> p b k", k=4)
                        g3 = Z["ig"][:, :n].rearrange("p (b k) -> p b k", k=4)
                        r3 = Z["r"][:, 0:NSEQ].unsqueeze(2)
                        tt(r3, a3[:, :, 0:1], h0s[:, c, :].unsqueeze(2), ALU.mult, [B["a"], b_h0s, B["r"]], [B["r"]])
                        tt(g3[:, :, 0:1], g3[:, :, 0:1], r3, ALU.add, [B["ig"], B["r"]], [B["ig"]])
                        S.op("dve", lambda: nc.vector.memset(a3[:, :, 0:1], 0.0), reads=[B["r"]], writes=[B["a"]])
                        S.op("dve", lambda: nc.vector.tensor_tensor_scan(Z["hc"][:, :n], Z["a"][:, :n], Z["ig"][:, :n],
                                                                         0.0, ALU.mult, ALU.add),
                             reads=[B["a"], B["ig"]], writes=[B["hc"]])
                        cp(hs_o[:, c, :].unsqueeze(2), Z["hc"][:, :n].rearrange("p (b k) -> p b k", k=4)[:, :, 3:4],
                           [B["hc"]], [b_hs_o])
                    tt(yb_[:, c, :n], Z["hc"][:, :n], Z["gate"][:, :n], ALU.mult, [B["hc"], B["gate"]], [by_])

                def rg_state_outputs():
                    with nc.allow_non_contiguous_dma(reason="small state out"):
                        for k in range(3):
                            S.dma("sp", O["conv_p"][k].rearrange("(c p) -> p c", p=128), halo[:, :, k], reads=[b_halo])
                        S.dma("sp", O["h_p"].rearrange("(c p) -> p c", p=128), hcar[:], reads=[b_hcar])
                    for (src_t, bsrc, rows, dst) in [(convs_o, b_convs_o, 48, O["conv_s"]), (hs_o, b_hs_o, 16, O["h_s"])]:
                        for hh in range(2):
                            bk = nb()
                            for j_ in range(4):
                                c = 4 * hh + j_
                                transpose_to(bk, j_ * 128, src_t[:, c, :], 128, rows, [bsrc], inc=(j_ == 3))
                            cp(stout[:rows, 512 * hh:512 * hh + 512], ps[:rows, bk, :], [psb[bk]], [b_stout])
                        S.dma("sp", dst, stout[:rows, :], reads=[b_stout])

                jobs = [(t, c) for t in range(5) for c in range(8)]
                norm(0, gmix[:, l, :], xnT[0], b_xnT[0])
                P1(0, 0, sets[0])
                for n_, (t, c) in enumerate(jobs):
                    nxt = jobs[n_ + 1] if n_ + 1 < len(jobs) else None
                    if nxt is not None and nxt[1] != 0:
                        P1(nxt[0], nxt[1], sets[(n_ + 1) % NSET])
                    if nxt is not None and nxt[1] == 0:
                        norm(nxt[0], gmix[:, l, :], xnT[0], b_xnT[0])
                    P2(t, c, sets[n_ % NSET])
                    if c == 7:
                        if nxt is not None:
                            P1(nxt[0], nxt[1], sets[(n_ + 1) % NSET])
                        else:
                            rg_state_outputs()
                        out_proj(t, Wo, bWo, ybuf[0], b_y[0])
                wissue(wb0 + 6)
                S.barrier()

        def s5(l, j, wb0):
            ELL = 4
            L = 32
            TS5 = [(i * 128, 128) for i in range(16)] + [(2048, 64)]
            fslot = (wb0 + 3) % 4
            with ExitStack() as st:
                def PT(name, shape=(128, 4, 8), dt=F32):
                    return T(st, name, list(shape), dt)
                lr = s5p["lr"]; li = s5p["li"]; dtt = s5p["dtt"]; th = PT("th"); rho = PT("rho")
                c1 = PT("c1"); s1 = PT("s1"); qre = PT("qre"); qim = PT("qim")
                t1 = PT("t1"); t2 = PT("t2"); ti = PT("ti", dt=I32)
                pw = T(st, "pw", [128, ELL + 1, 2, 32], F32)
                Xu = [T(st, "Xu%d" % i, [128, 32], F32) for i in range(2)]
                Xs = [T(st, "Xs%d" % i, [128, 32], F32) for i in range(2)]
                dcol = s5p["dcol"]
                id32 = T(st, "id32", [128, 32], F32)
                b_p = b_s5p; b_Xc = S.buf(); b_dD = S.buf()
                cosT = T(st, "cosT", [128, 32, L], F32); sinT = T(st, "sinT", [128, 32, L], F32)
                rhoT = T(st, "rhoT", [128, 32, L], F32)
                b_cos = S.buf(); b_sin = S.buf(); b_rho = S.buf()
                TB = [b_cos, b_sin, b_rho]
                tA = T(st, "tA", [128, 32, L], F32); tB = T(st, "tB", [128, 32, L], F32); b_tA = S.buf(); b_tB = S.buf()
                wre = T(st, "wre", [128, 32, L], F32); wim = T(st, "wim", [128, 32, L], F32); b_wre = S.buf(); b_wim = S.buf()
                CT = T(st, "CT", [128, 2 * ELL, 8, 128], BF16); b_CT = S.buf()
                KT = T(st, "KT", [128, ELL, 8, 32], BF16); b_KT = S.buf()
                BBTv = W[:, fslot].rearrange("p kc f -> p (kc f)")
                b_BBT = bW[fslot]

                def BBT(i, ri):
                    o = (2 * i + ri) * 1024
                    return BBTv[:, o:o + 1024].rearrange("p (c m) -> p c m", c=8)
                ubfL = [T(st, "ubf%d" % i, [128, 8, 128], BF16) for i in range(2)]; b_ubfL = S.bufs(2)
                sp2 = T(st, "sp2", [128, 8, 256], BF16); b_sp2 = S.buf()
                spL = [sq[:, :, 256:512], sp2[:, :, :]]; b_spL = [b_sq, b_sp2]
                xnT = T(st, "xnTc", [128, 8, 128], BF16); b_xnT = S.buf()
                gb = T(st, "gb5", [128, 8, 128], BF16); b_gb = S.buf()
                bxs = S.bufs(len(TS5))
                glu_a = rs2; b_glu = b_rs2
                flat = lambda tl: tl[:].rearrange("p k t -> p (k t)")
                X0 = [flat(cosT)[:, 0:512].rearrange("p (k b) -> p k b", b=NSEQ),
                      flat(sinT)[:, 0:512].rearrange("p (k b) -> p k b", b=NSEQ)]
                b_X0L = [b_cos, b_sin]
                c8 = lambda ap: ap.rearrange("p (c f) -> p c f", c=8)
                ZBre, b_ZBre = c8(flat(wre)), b_wre
                ZBim, b_ZBim = c8(flat(wim)), b_wim
                CTf = [c8(flat(tA)), c8(flat(tB))]; b_CTf = [b_tA, b_tB]
                Ere, b_Ere = c8(flat(cosT)), b_cos
                Eim, b_Eim = c8(flat(sinT)), b_sin
                rhoTf = flat(rhoT)
                bbr = rhoTf[:, 0:512].rearrange("p (kk c h) -> p kk c h", kk=4, c=8)
                bbi = rhoTf[:, 512:1024].rearrange("p (kk c h) -> p kk c h", kk=4, c=8)
                b_bb = b_rho
                sin_t = flat(tA)[0:16, :]; b_sin_t = b_tA
                wissue(wb0 + 2)
                pv = "(c kk g2) p -> (g2 p) kk c"
                for kk in range(4):
                    with nc.allow_non_contiguous_dma(reason="b load"):
                        S.dma("sp", bbr[:, kk], I["b_re"][j].rearrange("(c kk g2) p h -> (g2 p) kk c h", kk=4, g2=2)[:, kk], writes=[b_bb])
                        S.dma("sp", bbi[:, kk], I["b_im"][j].rearrange("(c kk g2) p h -> (g2 p) kk c h", kk=4, g2=2)[:, kk], writes=[b_bb])
                for ri, (nm, E_, bE_) in enumerate([("c_re", Ere, b_Ere), ("c_im", Eim, b_Eim)]):
                    S.op("dve", lambda: nc.vector.memset(E_, 0.0), writes=[bE_])
                    srcv = I[nm][j].rearrange("(c kk g2) h p -> kk g2 h c p", kk=4, g2=2)
                    for kk in range(4):
                        for g2 in range(2):
                            S.dma("sp", E_[32 * kk + 16 * g2:32 * kk + 16 * g2 + 16, :, 64 * g2:64 * g2 + 64],
                                  srcv[kk, g2], writes=[bE_])
                P_ = [b_p]
                act(dtt[:], dtt[:], AF.Exp, P_, P_)
                tt(th[:], li[:], dtt[:], ALU.mult, P_, P_)
                tt(t1[:], lr[:], dtt[:], ALU.mult, P_, P_)
                act(rho[:], t1[:], AF.Exp, P_, P_)

                def sin_of(out, ang, shift):
                    ts(t1[:], ang, 1.0 / (2 * np.pi), shift, ALU.mult, ALU.add, P_, P_)
                    cp(ti[:], t1[:], P_, P_)
                    cp(t2[:], ti[:], P_, P_)
                    tt(t1[:], t1[:], t2[:], ALU.subtract, P_, P_)
                    ts(t2[:], t1[:], 0.5, None, ALU.is_gt, None, P_, P_)
                    tt(t1[:], t1[:], t2[:], ALU.subtract, P_, P_)
                    ts(t2[:], t1[:], -0.5, None, ALU.is_lt, None, P_, P_)
                    tt(t1[:], t1[:], t2[:], ALU.add, P_, P_)
                    act(out, t1[:], AF.Sin, P_, P_, scale=6.283185)
                sin_of(s1[:], th[:], 0.0)
                sin_of(c1[:], th[:], 0.25)
                f2 = "p kk c -> p (kk c)"
                S.op("dve", lambda: nc.vector.memset(pw[:, 0, 0, :], 1.0), writes=P_)
                S.op("dve", lambda: nc.vector.memset(pw[:, 0, 1, :], 0.0), writes=P_)
                tt(pw[:, 1, 0, :], rho[:].rearrange(f2), c1[:].rearrange(f2), ALU.mult, P_, P_)
                tt(pw[:, 1, 1, :], rho[:].rearrange(f2), s1[:].rearrange(f2), ALU.mult, P_, P_)
                t1f_, t2f_ = t1[:].rearrange(f2), t2[:].rearrange(f2)
                for k in range(2, ELL + 1):
                    tt(t1f_, pw[:, k - 1, 0, :], pw[:, 1, 0, :], ALU.mult, P_, P_)
                    tt(t2f_, pw[:, k - 1, 1, :], pw[:, 1, 1, :], ALU.mult, P_, P_)
                    tt(pw[:, k, 0, :], t1f_, t2f_, ALU.subtract, P_, P_)
                    tt(t1f_, pw[:, k - 1, 0, :], pw[:, 1, 1, :], ALU.mult, P_, P_)
                    tt(t2f_, pw[:, k - 1, 1, :], pw[:, 1, 0, :], ALU.mult, P_, P_)
                    tt(pw[:, k, 1, :], t1f_, t2f_, ALU.add, P_, P_)
                tt(t1[:], rho[:], c1[:], ALU.mult, P_, P_)
                ts(t1[:], t1[:], -1.0, None, ALU.add, None, P_, P_)
                tt(t2[:], rho[:], s1[:], ALU.mult, P_, P_)
                t3 = dtt
                tt(t3[:], lr[:], lr[:], ALU.mult, P_, P_)
                tt(qre[:], li[:], li[:], ALU.mult, P_, P_)
                tt(t3[:], t3[:], qre[:], ALU.add, P_, P_)
                S.op("dve", lambda: nc.vector.reciprocal(t3[:], t3[:]), reads=P_, writes=P_)
                tt(qre[:], t1[:], lr[:], ALU.mult, P_, P_)
                tt(qim[:], t2[:], li[:], ALU.mult, P_, P_)
                tt(qre[:], qre[:], qim[:], ALU.add, P_, P_)
                tt(qim[:], t2[:], lr[:], ALU.mult, P_, P_)
                tt(t2[:], t1[:], li[:], ALU.mult, P_, P_)
                tt(qim[:], qim[:], t2[:], ALU.subtract, P_, P_)
                tt(qre[:], qre[:], t3[:], ALU.mult, P_, P_)
                tt(qim[:], qim[:], t3[:], ALU.mult, P_, P_)
                tA4 = tA[:, :, 0:16].rearrange("p (kk c) h -> p kk c h", kk=4)
                tB4 = tB[:, :, 0:16].rearrange("p (kk c) h -> p kk c h", kk=4)
                tC4 = tA[:, :, 16:32].rearrange("p (kk c) h -> p kk c h", kk=4)
                tD4 = tB[:, :, 16:32].rearrange("p (kk c) h -> p kk c h", kk=4)
                bc4 = lambda ap32: ap32.rearrange("p (kk c) -> p kk c", kk=4).unsqueeze(3).to_broadcast([128, 4, 8, 16])
                qr_b, qi_b = bc4(qre[:].rearrange(f2)), bc4(qim[:].rearrange(f2))
                tt(tA4, bbr, qr_b, ALU.mult, [b_bb] + P_, [b_tA])
                tt(tB4, bbi, qi_b, ALU.mult, [b_bb] + P_, [b_tB])
                tt(tC4, bbi, qr_b, ALU.mult, [b_bb] + P_, [b_tA])
                tt(tD4, bbr, qi_b, ALU.mult, [b_bb] + P_, [b_tB])
                tt(bbr, tA4, tB4, ALU.subtract, [b_tA, b_tB], [b_bb])
                tt(bbi, tC4, tD4, ALU.add, [b_tA, b_tB], [b_bb])
                S.op("dve", lambda: nc.vector.memset(ZBre, 0.0), writes=[b_ZBre])
                S.op("dve", lambda: nc.vector.memset(ZBim, 0.0), writes=[b_ZBim])
                for i in range(ELL):
                    k = ELL - 1 - i
                    pr_b, pi_b = bc4(pw[:, k, 0, :]), bc4(pw[:, k, 1, :])
                    for ri in range(2):
                        if ri == 0:
                            tt(tA4, bbr, pr_b, ALU.mult, [b_bb] + P_, [b_tA])
                            tt(tB4, bbi, pi_b, ALU.mult, [b_bb] + P_, [b_tB])
                            op, ZB_, bZB_ = ALU.subtract, ZBre, b_ZBre
                        else:
                            tt(tA4, bbi, pr_b, ALU.mult, [b_bb] + P_, [b_tA])
                            tt(tB4, bbr, pi_b, ALU.mult, [b_bb] + P_, [b_tB])
                            op, ZB_, bZB_ = ALU.add, ZBim, b_ZBim
                        for g2 in range(2):
                            zv = ZB_[64 * g2:64 * g2 + 64].rearrange("p c (kk g h) -> p kk c g h", kk=4, g=2)[:, :, :, g2, :]
                            tt(zv, tA4[64 * g2:64 * g2 + 64], tB4[64 * g2:64 * g2 + 64], op, [b_tA, b_tB], [bZB_])
                        for hh in range(2):
                            bk = nb()
                            for j_ in range(4):
                                transpose_to(bk, j_ * 128, ZB_[:, 4 * hh + j_, :], 128, 128, [bZB_], inc=(j_ == 3))
                            act(BBT(i, ri)[:, 4 * hh:4 * hh + 4, :], ps[:, bk, :].rearrange("p (j t) -> p j t", j=4),
                                AF.Copy, [psb[bk]], [b_BBT])
                ts(ZBim, ZBim, -1.0, None, ALU.mult, None, [b_ZBim], [b_ZBim])
                for ri, (E_, bE_) in enumerate([(Ere, b_Ere), (Eim, b_Eim)]):
                    for hh in range(2):
                        bk = nb()
                        for j_ in range(4):
                            transpose_to(bk, j_ * 128, E_[:, 4 * hh + j_, :], 128, 128, [bE_], inc=(j_ == 3))
                        act(CTf[ri][:, 4 * hh:4 * hh + 4, :], ps[:, bk, :].rearrange("p (j t) -> p j t", j=4),
                            AF.Copy, [psb[bk]], [b_CTf[ri]])
                cp(id32[:], ident[:, 0:32], [b_ident], [b_dD])
                for kk in range(1, 4):
                    tt(id32[:], id32[:], ident[:, 32 * kk:32 * kk + 32], ALU.add, [b_ident, b_dD], [b_dD])
                rT = c8(rhoTf)
                b_rT = b_rho
                e4 = lambda ap: ap.rearrange("p c (kk m) -> p c kk m", kk=4)
                for k in range(ELL + 1):
                    pk = lambda ri_: pw[:, k, ri_, :].rearrange("p (kk c) -> p c kk", kk=4).unsqueeze(3).to_broadcast([128, 8, 4, 32])
                    tt(e4(Ere), e4(CTf[0]), pk(0), ALU.mult, [b_CTf[0]] + P_, [b_Ere])
                    tt(e4(rT), e4(CTf[1]), pk(1), ALU.mult, [b_CTf[1]] + P_, [b_rT])
                    tt(Ere, Ere, rT, ALU.subtract, [b_Ere, b_rT], [b_Ere])
                    tt(e4(Eim), e4(CTf[0]), pk(1), ALU.mult, [b_CTf[0]] + P_, [b_Eim])
                    tt(e4(rT), e4(CTf[1]), pk(0), ALU.mult, [b_CTf[1]] + P_, [b_rT])
                    tt(Eim, Eim, rT, ALU.add, [b_Eim, b_rT], [b_Eim])
                    if k >= 1:
                        act(CT[:, 2 * (k - 1), :, :], Ere, AF.Copy, [b_Ere], [b_CT])
                        ts(CT[:, 2 * (k - 1) + 1, :, :], Eim, -1.0, None, ALU.mult, None, [b_Eim], [b_CT])
                    if k <= ELL - 1:
                        bk = nb(2)
                        pk2 = ps[:, bk:bk + 2, :].rearrange("p b f -> p (b f)").rearrange("p (c m) -> p c m", c=8)
                        for c in range(8):
                            bkc = bk + c // 4
                            S.op("pe", lambda: nc.tensor.matmul(pk2[:, c, :], lhsT=ZBre[:, c, :], rhs=Ere[:, c, :],
                                                                start=True, stop=False),
                                 reads=[b_ZBre, b_Ere], writes=[psb[bkc]], inc=False)
                            S.op("pe", lambda: nc.tensor.matmul(pk2[:, c, :], lhsT=ZBim[:, c, :], rhs=Eim[:, c, :],
                                                                start=False, stop=True),
                                 reads=[b_ZBim, b_Eim], writes=[psb[bkc]], inc=(c % 4 == 3))
                        for kk in range(4):
                            blk = pk2[32 * kk:32 * kk + 32, :, 32 * kk:32 * kk + 32]
                            if k == 0:
                                tt(KT[32 * kk:32 * kk + 32, k, :, :],
                                   id32[32 * kk:32 * kk + 32, :].unsqueeze(1).to_broadcast([32, 8, 32]),
                                   dcol[32 * kk:32 * kk + 32, :].unsqueeze(2).to_broadcast([32, 8, 32]), ALU.mult,
                                   [b_dD], [b_KT])
                                tt(KT[32 * kk:32 * kk + 32, k, :, :], KT[32 * kk:32 * kk + 32, k, :, :], blk, ALU.add,
                                   [psb[bk], psb[bk + 1], b_KT], [b_KT])
                            else:
                                cp(KT[32 * kk:32 * kk + 32, k, :, :], blk, [psb[bk], psb[bk + 1]], [b_KT])
                rho4 = qre
                tt(rho4[:], rho[:], rho[:], ALU.mult, P_, P_)
                tt(rho4[:], rho4[:], rho4[:], ALU.mult, P_, P_)
                for _ in range(2):
                    tt(t1[:], c1[:], c1[:], ALU.mult, P_, P_)
                    tt(t2[:], s1[:], s1[:], ALU.mult, P_, P_)
                    tt(s1[:], c1[:], s1[:], ALU.mult, P_, P_)
                    ts(s1[:], s1[:], 2.0, None, ALU.mult, None, P_, P_)
                    tt(c1[:], t1[:], t2[:], ALU.subtract, P_, P_)
                cp(cosT[:, :, 0:1], c1[:].rearrange(f2).unsqueeze(2), P_ + [b_KT, b_CT], [b_cos])
                cp(sinT[:, :, 0:1], s1[:].rearrange(f2).unsqueeze(2), P_ + [b_KT, b_CT], [b_sin])
                m = 1
                while m < L:
                    pr = cosT[:, :, m - 1:m].to_broadcast([128, 32, m])
                    pi_ = sinT[:, :, m - 1:m].to_broadcast([128, 32, m])
                    tt(tA[:, :, :m], cosT[:, :, 0:m], pr, ALU.mult, TB, [b_tA])
                    tt(tB[:, :, :m], sinT[:, :, 0:m], pi_, ALU.mult, TB, [b_tB])
                    tt(cosT[:, :, m:2 * m], tA[:, :, :m], tB[:, :, :m], ALU.subtract, [b_tA, b_tB], [b_cos])
                    tt(tA[:, :, :m], cosT[:, :, 0:m], pi_, ALU.mult, TB, [b_tA])
                    tt(tB[:, :, :m], sinT[:, :, 0:m], pr, ALU.mult, TB, [b_tB])
                    tt(sinT[:, :, m:2 * m], tA[:, :, :m], tB[:, :, :m], ALU.add, [b_tA, b_tB], [b_sin])
                    m *= 2
                rho4f = rho4[:].rearrange(f2)
                cp(rhoT[:], rho4f.unsqueeze(2).to_broadcast([128, 32, L]), P_ + [b_rT], [b_rho])
                S.op("dve", lambda: nc.vector.memset(rhoT[:, :, 0:1], 0.0), reads=[b_rho], writes=[b_rho])
                for ri in range(2):
                    S.op("dve", lambda: nc.vector.memset(Xu[ri][:], 0.0), writes=[b_Xc])
                    S.op("dve", lambda: nc.vector.memset(Xs[ri][:], 0.0), writes=[b_Xc])

                Wi, Wga, Wgb = W[:, wb0 % 4], W[:, (wb0 + 1) % 4], W[:, (wb0 + 2) % 4]
                bWi, bWga, bWgb = bW[wb0 % 4], bW[(wb0 + 1) % 4], bW[(wb0 + 2) % 4]
                V4 = lambda ap: ap.rearrange("p (kk c) t -> p kk (c t)", kk=4)
                K4 = lambda ap: ap.rearrange("p (kk c) t -> p kk c t", kk=4)
                def load_X0():
                    for ri, nm in enumerate(["st_re", "st_im"]):
                        bk = nb()
                        for q4 in range(4):
                            S.dma("sp", sin_t[:, :], I[nm][:, 1024 * q4:1024 * q4 + 1024], writes=[b_sin_t])
                            for k8 in range(8):
                                k = q4 * 8 + k8
                                transpose_to(bk, k * 16, sin_t[:16, 128 * k8:128 * k8 + 128], 16, 128, [b_sin_t], inc=(k8 == 7))
                        cp(X0[ri].rearrange("p (kk c) b -> p kk c b", kk=4),
                           ps[:, bk, :].rearrange("p (c kk b) -> p kk c b", kk=4, b=16), [psb[bk]], [b_X0L[ri]])

                def stA(it):
                    t0, n = TS5[it]
                    smp = (it == 16)
                    tglob = 4 if smp else it // 4
                    nsb = n // ELL
                    par = it % 2
                    ubf, b_ubf = ubfL[par], b_ubfL[par]
                    spv, b_sp = spL[par], b_spL[par]
                    xrb4 = spv[:, :, 0:128].rearrange("p c (kk t) -> p kk c t", kk=4)
                    xib4 = spv[:, :, 128:256].rearrange("p c (kk t) -> p kk c t", kk=4)
                    ub3 = lambda kk_, c_, i_: ubf[32 * kk_:32 * kk_ + 32, c_, i_ * nsb:(i_ + 1) * nsb]
                    for fc in range(8):
                        bk = nb()
                        mm_acc(ps[:, bk, :n], [(Wi[:, kc, fc * 128:(fc + 1) * 128], xnT[:, kc, :n]) for kc in range(8)],
                               [bWi, b_xnT], psb[bk])
                        act(ubf[:, fc, :n].rearrange("p (i s) -> p s i", i=ELL),
                            ps[:, bk, :n].rearrange("p (s i) -> p s i", i=ELL), AF.Copy, [psb[bk]], [b_ubf])
                    stA2(it, smp, nsb, n, ubf, b_ubf, b_sp, xrb4, xib4, ub3)

                def stN(it):
                    t0, n = TS5[it]
                    act(sq[:, :, :n], x[:, :, t0:t0 + n], AF.Square, [bxs[it]], [b_sq])
                    bk = nb()
                    mm_acc(ps[:, bk, :n], [(onesb[:, :], sq[:, c, :n]) for c in range(8)], [b_sq, b_onesb], psb[bk])
                    act(rs[:, :n], ps[:, bk, :n], AF.Sqrt, [psb[bk], b_cst], [b_rs], scale=1.0 / D, bias=cst[:, 0:1])
                    S.op("dve", lambda: nc.vector.reciprocal(rs2[:, :n], rs[:, :n]), reads=[b_rs], writes=[b_rs2])
                    for c in range(8):
                        stt(xnT[:, c, :n], x[:, c, t0:t0 + n], gmix[:, l, c:c + 1], rs2[:, :n], ALU.mult, ALU.mult,
                            [bxs[it], b_rs2, b_g], [b_xnT])

                def stA2(it, smp, nsb, n, ubf, b_ubf, b_sp, xrb4, xib4, ub3):
                    bk0 = nb()
                    while bk0 % 4 != 0:
                        bk0 = nb()
                    for _ in range(3):
                        nb()
                    for kk in range(4):
                        for ri in range(2):
                            for c in range(8):
                                for i in range(ELL):
                                    last = (ri == 1 and c == 7 and i == ELL - 1)
                                    S.op("pe", lambda: nc.tensor.matmul(
                                        ps[:, bk0 + kk, ri * 256 + c * L: ri * 256 + c * L + nsb],
                                        lhsT=BBT(i, ri)[32 * kk:32 * kk + 32, c, :], rhs=ub3(kk, c, i),
                                        start=(i == 0), stop=(i == ELL - 1), tile_position=(32 * kk, 0),
                                        skip_group_check=True),
                                        reads=[b_BBT, b_ubf], writes=[psb[bk0 + kk]], inc=last)
                    pb = [psb[bk0 + kk] for kk in range(4)]
                    pre = ps[:, bk0:bk0 + 4, 0:256]
                    pim = ps[:, bk0:bk0 + 4, 256:512]
                    if not smp:
                        tt(V4(wre[:]), pre, V4(cosT[:]), ALU.mult, pb + TB, [b_wre])
                        tt(V4(tA[:]), pim, V4(sinT[:]), ALU.mult, pb + TB, [b_tA])
                        tt(wre[:], wre[:], tA[:], ALU.add, [b_wre, b_tA], [b_wre])
                        tt(V4(wim[:]), pim, V4(cosT[:]), ALU.mult, pb + TB, [b_wim])
                        tt(V4(tA[:]), pre, V4(sinT[:]), ALU.mult, pb + TB, [b_tA])
                        tt(wim[:], wim[:], tA[:], ALU.subtract, [b_wim, b_tA], [b_wim])
                        if it > 0:
                            tt(wre[:, :, 0:1], wre[:, :, 0:1], Xs[0][:].unsqueeze(2), ALU.add, [b_wre, b_Xc], [b_wre])
                            tt(wim[:, :, 0:1], wim[:, :, 0:1], Xs[1][:].unsqueeze(2), ALU.add, [b_wim, b_Xc], [b_wim])
                        cp(xrb4[:, :, :, 0:1], Xu[0][:].rearrange("p (kk c) -> p kk c", kk=4).unsqueeze(3), [b_Xc], [b_sp])
                        cp(xib4[:, :, :, 0:1], Xu[1][:].rearrange("p (kk c) -> p kk c", kk=4).unsqueeze(3), [b_Xc], [b_sp])
                        for (w_, bw_) in ((wre, b_wre), (wim, b_wim)):
                            wf = flat(w_)
                            S.op("dve", lambda: nc.vector.tensor_tensor_scan(wf, flat(rhoT), wf, 0.0, ALU.mult, ALU.add),
                                 reads=[bw_] + TB, writes=[bw_])
                        tt(tA[:], wre[:], cosT[:], ALU.mult, [b_wre] + TB, [b_tA])
                        tt(tB[:], wim[:], sinT[:], ALU.mult, [b_wim] + TB, [b_tB])
                        tt(xrb4[:, :, :, 1:L], K4(tA[:])[:, :, :, 0:L - 1], K4(tB[:])[:, :, :, 0:L - 1], ALU.subtract,
                           [b_tA, b_tB], [b_sp])
                        tt(Xu[0][:].unsqueeze(2), tA[:, :, L - 1:L], tB[:, :, L - 1:L], ALU.subtract, [b_tA, b_tB], [b_Xc])
                        tt(tA[:], wre[:], sinT[:], ALU.mult, [b_wre] + TB, [b_tA])
                        tt(tB[:], wim[:], cosT[:], ALU.mult, [b_wim] + TB, [b_tB])
                        tt(xib4[:, :, :, 1:L], K4(tA[:])[:, :, :, 0:L - 1], K4(tB[:])[:, :, :, 0:L - 1], ALU.add,
                           [b_tA, b_tB], [b_sp])
                        tt(Xu[1][:].unsqueeze(2), tA[:, :, L - 1:L], tB[:, :, L - 1:L], ALU.add, [b_tA, b_tB], [b_Xc])
                        tt(Xs[0][:], Xu[0][:], rho4f, ALU.mult, [b_Xc] + P_, [b_Xc])
                        tt(Xs[1][:], Xu[1][:], rho4f, ALU.mult, [b_Xc] + P_, [b_Xc])
                    else:
                        load_X0()
                        cp(xrb4[:, :, :, 0:NSEQ], X0[0].rearrange("p (kk c) b -> p kk c b", kk=4), b_X0L, [b_sp])
                        cp(xib4[:, :, :, 0:NSEQ], X0[1].rearrange("p (kk c) b -> p kk c b", kk=4), b_X0L, [b_sp])
                        p4r = pw[:, ELL, 0, :].unsqueeze(2).to_broadcast([128, 32, NSEQ])
                        p4i = pw[:, ELL, 1, :].unsqueeze(2).to_broadcast([128, 32, NSEQ])
                        vre = ps[:, bk0:bk0 + 4, 0:256].rearrange("p kk (c t) -> p kk c t", c=8)[:, :, :, 0:NSEQ]
                        vim = ps[:, bk0:bk0 + 4, 256:512].rearrange("p kk (c t) -> p kk c t", c=8)[:, :, :, 0:NSEQ]
                        tt(tA[:, :, 0:NSEQ], X0[0], p4r, ALU.mult, b_X0L + P_, [b_tA])
                        tt(tB[:, :, 0:NSEQ], X0[1], p4i, ALU.mult, b_X0L + P_, [b_tB])
                        tt(tA[:, :, 0:NSEQ], tA[:, :, 0:NSEQ], tB[:, :, 0:NSEQ], ALU.subtract, [b_tA, b_tB], [b_tA])
                        tt(K4(wre[:])[:, :, :, 0:NSEQ], vre, K4(tA[:])[:, :, :, 0:NSEQ], ALU.add, pb + [b_tA], [b_wre])
                        tt(tA[:, :, 0:NSEQ], X0[0], p4i, ALU.mult, b_X0L + P_, [b_tA])
                        tt(tB[:, :, 0:NSEQ], X0[1], p4r, ALU.mult, b_X0L + P_, [b_tB])
                        tt(tA[:, :, 0:NSEQ], tA[:, :, 0:NSEQ], tB[:, :, 0:NSEQ], ALU.add, [b_tA, b_tB], [b_tA])
                        tt(K4(wim[:])[:, :, :, 0:NSEQ], vim, K4(tA[:])[:, :, :, 0:NSEQ], ALU.add, pb + [b_tA], [b_wim])

                def stB(it):
                    t0, n = TS5[it]
                    smp = (it == 16)
                    tglob = 4 if smp else it // 4
                    nsb = n // ELL
                    par = it % 2
                    ubf, b_ubf = ubfL[par], b_ubfL[par]
                    spv, b_sp = spL[par], b_spL[par]
                    xrb4 = spv[:, :, 0:128].rearrange("p c (kk t) -> p kk c t", kk=4)
                    xib4 = spv[:, :, 128:256].rearrange("p c (kk t) -> p kk c t", kk=4)
                    ub3 = lambda kk_, c_, i_: ubf[32 * kk_:32 * kk_ + 32, c_, i_ * nsb:(i_ + 1) * nsb]
                    started = set()
                    bkY = [nb(), nb()]
                    for half in range(2):
                        bk = bkY[half]
                        js = [2 * half, 2 * half + 1]
                        for j_ in js:
                            col0 = (j_ % 2) * 256
                            for c in range(8):
                                for kk in range(4):
                                    outp = ps[32 * kk:32 * kk + 32, bk, col0 + c * L: col0 + c * L + nsb]
                                    first = (bk, kk) not in started
                                    started.add((bk, kk))
                                    S.op("pe", lambda: nc.tensor.matmul(
                                        outp, lhsT=CT[:, 2 * j_, c, 32 * kk:32 * kk + 32],
                                        rhs=spv[:, c, kk * 32:kk * 32 + nsb],
                                        start=first, stop=False, tile_position=(0, 32 * kk), skip_group_check=True),
                                        reads=[b_CT, b_sp], writes=[psb[bk]], inc=False)
                                    S.op("pe", lambda: nc.tensor.matmul(
                                        outp, lhsT=CT[:, 2 * j_ + 1, c, 32 * kk:32 * kk + 32],
                                        rhs=spv[:, c, 128 + kk * 32:128 + kk * 32 + nsb],
                                        start=False, stop=False, tile_position=(0, 32 * kk), skip_group_check=True),
                                        reads=[b_CT, b_sp], writes=[psb[bk]], inc=False)
                        for j_ in js:
                            col0 = (j_ % 2) * 256
                            for c in range(8):
                                for kk in range(4):
                                    outp = ps[32 * kk:32 * kk + 32, bk, col0 + c * L: col0 + c * L + nsb]
                                    for i in range(j_ + 1):
                                        fin = (j_ % 2 == 1 and c == 7 and kk == 3 and i == j_)
                                        S.op("pe", lambda: nc.tensor.matmul(
                                            outp, lhsT=KT[32 * kk:32 * kk + 32, j_ - i, c, :], rhs=ub3(kk, c, i),
                                            start=False, stop=True, tile_position=(32 * kk, 32 * kk), skip_group_check=True),
                                            reads=[b_KT, b_ubf], writes=[psb[bk]], inc=fin)
                    for j_ in range(ELL):
                        bk = bkY[j_ // 2]
                        col0 = (j_ % 2) * 256
                        act(gb[:, :, :n].rearrange("p c (s i) -> p c s i", i=ELL)[:, :, :, j_],
                            ps[:, bk, col0:col0 + 256].rearrange("p (c s) -> p c s", c=8)[:, :, :nsb], AF.Gelu,
                            [psb[bk]], [b_gb])
                    for oc in range(8):
                        bka = nb()
                        mm_acc(ps[:, bka, :n], [(Wga[:, fc, oc * 128:(oc + 1) * 128], gb[:, fc, :n]) for fc in range(8)],
                               [bWga, b_gb], psb[bka])
                        bkb = nb()
                        mm_acc(ps[:, bkb, :n], [(Wgb[:, fc, oc * 128:(oc + 1) * 128], gb[:, fc, :n]) for fc in range(8)],
                               [bWgb, b_gb], psb[bkb])
                        act(glu_a[:, :n], ps[:, bkb, :n], AF.Sigmoid, [psb[bkb]], [b_glu])
                        tt(glu_a[:, :n], ps[:, bka, :n], glu_a[:, :n], ALU.mult, [psb[bka], b_glu], [b_glu])
                        tt(x[:, oc, t0:t0 + n], x[:, oc, t0:t0 + n], glu_a[:, :n], ALU.add, [bxs[it], b_glu], [bxs[it]])

                def s5_state_outputs():
                    pvo = "(c kk q) -> q kk c"
                    with nc.allow_non_contiguous_dma(reason="state out"):
                        for kk in range(4):
                            S.dma("sp", O["sre_p"].rearrange(pvo, kk=4, q=128)[:, kk, :], Xu[0][:, 8 * kk:8 * kk + 8], reads=[b_Xc])
                            S.dma("sp", O["sim_p"].rearrange(pvo, kk=4, q=128)[:, kk, :], Xu[1][:, 8 * kk:8 * kk + 8], reads=[b_Xc])
                    for ri, (dst, src_t, bsrc) in enumerate([(O["sre_s"], wre, b_wre), (O["sim_s"], wim, b_wim)]):
                        for q4 in range(8):
                            bk = nb()
                            for j_ in range(4):
                                k = q4 * 4 + j_
                                c_, kk_ = k // 4, k % 4
                                transpose_to(bk, j_ * 128, src_t[:, kk_ * 8 + c_, 0:NSEQ], 128, 16, [bsrc], inc=(j_ == 3))
                            cp(sin_t[:16, 512 * (q4 % 2):512 * (q4 % 2) + 512], ps[:16, bk, :], [psb[bk]], [b_sin_t])
                            if q4 % 2 == 1:
                                S.dma("sp", dst[:, 1024 * (q4 // 2):1024 * (q4 // 2) + 1024], sin_t[:16, :], reads=[b_sin_t])

                NT5 = len(TS5)
                stN(0)
                stA(0)
                stN(1)
                for it in range(NT5):
                    if it + 1 < NT5:
                        stA(it + 1)
                        if it + 1 == NT5 - 1:
                            wissue(wb0 + 4)
                            s5_state_outputs()
                    if it + 2 < NT5:
                        stN(it + 2)
                    stB(it)
                wissue(wb0 + 6)
                S.barrier()

        for l in range(depth):
            kind = l % 3
            j = l // 3
            wb0 = 11 * l
            if kind == 0:
                sgu(l, j, wb0)
            elif kind == 1:
                rglru(l, j, wb0)
            else:
                s5(l, j, wb0)
            ffn(l, wb0 + 3)

        with ExitStack() as st:
            xf = [T(st, "xf%d" % i, [128, 8, 512], F32) for i in range(2)]; b_xf = S.bufs(2)
            yo = [T(st, "yo%d" % i, [128, 1024], F32) for i in range(4)]; b_yo = S.bufs(4)
            oc_ = [0]
            norm(0, gfin, xf[0], b_xf[0])
            for t in range(5):
                t0, n = TILES[t]
                if t + 1 < 5:
                    norm(t + 1, gfin, xf[(t + 1) % 2], b_xf[(t + 1) % 2])
                nblk = max(1, n // 128)
                rows = min(n, 128)
                for bi in range(nblk):
                    k_ = oc_[0] % 4
                    oc_[0] += 1
                    for hh in range(2):
                        bk = nb()
                        for j_ in range(4):
                            c = 4 * hh + j_
                            transpose_to(bk, j_ * 128, xf[t % 2][:, c, bi * 128:bi * 128 + rows], 128, rows,
                                         [b_xf[t % 2]], inc=(j_ == 3))
                        if hh == 0:
                            act(yo[k_][:rows, 0:512], ps[:rows, bk, :], AF.Copy, [psb[bk]], [b_yo[k_]])
                        else:
                            cp(yo[k_][:rows, 512:1024], ps[:rows, bk, :], [psb[bk]], [b_yo[k_]])
                    dst = O["y_p"][t0 + bi * 128:t0 + bi * 128 + rows, :] if t < 4 else O["y_s"][:, :]
                    S.dma("sp", dst, yo[k_][:rows, :], reads=[b_yo[k_]])
        S.finish("sp")
    return nc


_CACHE = {}


def kernel(**inputs):
    depth = int(os.environ.get("KDEPTH", "4"))
    if depth not in _CACHE:
        _CACHE[depth] = build(depth)
    nc = _CACHE[depth]
    f = lambda a: np.ascontiguousarray(np.asarray(a, dtype=np.float32))
    shared = {}
    for nm in ["norm_mix", "norm_ffn", "norm_f", "w_ff1", "w_ff2", "w_in_a", "sgu_g", "w_s", "b_s", "w_out_a",
               "w_in_b", "conv_w", "conv_b", "w_a", "b_a", "w_x", "b_x", "lam", "w_out_b", "w_in_c",
               "lam_re", "lam_im", "log_dt", "b_re", "b_im", "c_re", "c_im", "d_skip", "w_glu"]:
        if depth == 0 and nm.startswith("w_"):
            continue
        shared[nm] = f(inputs[nm])
    xp = f(inputs["x_prompt"]); xs = f(inputs["x_sample"])
    stc = f(inputs["state_rglru_conv"]); sth = f(inputs["state_rglru_h"])
    sre = f(inputs["state_s5_re"]); sim = f(inputs["state_s5_im"])
    in_maps = []
    for i in range(NCORES):
        sl = slice(i * NSEQ, (i + 1) * NSEQ)
        m = dict(shared)
        m["xp"] = xp[i]
        m["xs"] = xs[sl].reshape(NS, D)
        m["st_conv"] = stc[0, sl].reshape(NSEQ * 3, D)
        m["st_h"] = sth[0, sl].reshape(NSEQ, D)
        m["st_re"] = sre[0, sl].reshape(NSEQ, 4096)
        m["st_im"] = sim[0, sl].reshape(NSEQ, 4096)
        in_maps.append(m)
    res = run_bass_kernel_spmd(nc, in_maps, core_ids=list(range(NCORES)))
    R = res.results
    cat = lambda k, shp: np.stack([np.asarray(r[k], dtype=np.float32).reshape(shp) for r in R])
    y_p = cat("y_p", (NP, D))
    y_s = cat("y_s", (NSEQ, 4, D)).reshape(128, 4, D)
    v_s = np.concatenate([np.asarray(r["v_s"], dtype=np.float32).reshape(2, NSEQ, 4, D) for r in R], axis=1)
    conv_p = cat("conv_p", (3, D))[None]
    h_p = cat("h_p", (D,))[None]
    conv_s = cat("conv_s", (NSEQ, 3, D)).reshape(1, 128, 3, D)
    h_s = cat("h_s", (NSEQ, D)).reshape(1, 128, D)
    sre_p = cat("sre_p", (64, 64))[None]
    sim_p = cat("sim_p", (64, 64))[None]
    sre_s = cat("sre_s", (NSEQ, 64, 64)).reshape(1, 128, 64, 64)
    sim_s = cat("sim_s", (NSEQ, 64, 64)).reshape(1, 128, 64, 64)
    return (y_p, y_s, v_s, conv_p, h_p, conv_s, h_s, sre_p, sim_p, sre_s, sim_s)
```

```python
import os
import numpy as np
from contextlib import ExitStack
import concourse.bass as bass
import concourse.mybir as mybir
from concourse.bass_utils import run_bass_kernel_spmd

F32 = mybir.dt.float32
BF16 = mybir.dt.bfloat16
I32 = mybir.dt.int32
AF = mybir.ActivationFunctionType
ALU = mybir.AluOpType

NCORES = 8
D = 1024
NP = 2048
NSEQ = 16
NS = 64
NT = NP + NS
EPS = 1e-6
TILES = [(0, 512), (512, 512), (1024, 512), (1536, 512), (2048, 64)]


class Buf:
    __slots__ = ("name", "w", "r")

    def __init__(self, name):
        self.name = name
        self.w = None
        self.r = []


class Sched:
    NDMA = 16

    def __init__(self, nc, stack):
        self.nc = nc
        self.eng = {"pe": nc.tensor, "act": nc.scalar, "dve": nc.vector, "pool": nc.gpsimd, "sp": nc.sync}
        self.sem = {}
        self.cnt = {}
        for k in ["pe", "act", "dve", "pool"]:
            self.sem[k] = stack.enter_context(nc.semaphore("s_" + k))
            self.cnt[k] = 0
        self.dq = {}
        for q in ["sp", "pool"]:
            sems = []
            for i in range(self.NDMA):
                key = "d_%s_%d" % (q, i)
                self.sem[key] = stack.enter_context(nc.semaphore(key))
                self.cnt[key] = 0
                sems.append(key)
            self.dq[q] = [sems, 0]
        self.seen = {}
        self.nbuf = 0

    def buf(self, name=None):
        self.nbuf += 1
        return Buf(name or "b%d" % self.nbuf)

    def bufs(self, n):
        return [self.buf() for _ in range(n)]

    def _wait(self, e, needs):
        for key, val in needs.items():
            if val <= 0 or self.seen.get((e, key), 0) >= val:
                continue
            if e == "pe" and key == "pe":
                continue
            self.eng[e].wait_ge(self.sem[key], val)
            self.seen[(e, key)] = val

    def _needs(self, reads, writes):
        needs = {}

        def add(kv):
            if kv is not None and needs.get(kv[0], 0) < kv[1]:
                needs[kv[0]] = kv[1]
        for b in reads:
            add(b.w)
        for b in writes:
            add(b.w)
            for r in b.r:
                add(r)
        return needs

    def op(self, e, fn, reads=(), writes=(), inc=True):
        self._wait(e, self._needs(reads, writes))
        ins = fn()
        if inc:
            self.cnt[e] += 1
            ins.then_inc(self.sem[e], 1)
            tag = (e, self.cnt[e])
        else:
            tag = (e, self.cnt[e] + 1)
        for b in reads:
            b.r.append(tag)
            if len(b.r) > 64:
                b.r = self._compact(b.r)
        for b in writes:
            b.w = tag
            b.r = []
        return ins

    @staticmethod
    def _compact(lst):
        m = {}
        for k, v in lst:
            if m.get(k, 0) < v:
                m[k] = v
        return list(m.items())

    def dma(self, q, out, in_, reads=(), writes=(), **kw):
        sems, idx = self.dq[q]
        key = sems[idx % self.NDMA]
        self.dq[q][1] = idx + 1
        needs = self._needs(reads, writes)
        if self.cnt[key] > 0:
            needs[key] = max(needs.get(key, 0), self.cnt[key])
        self._wait(q, needs)
        ins = self.eng[q].dma_start(out=out, in_=in_, **kw)
        self.cnt[key] += 16
        ins.then_inc(self.sem[key], 16)
        tag = (key, self.cnt[key])
        for b in reads:
            b.r.append(tag)
        for b in writes:
            b.w = tag
            b.r = []
        return tag

    def finish(self, e="sp"):
        self._wait(e, {k: v for k, v in self.cnt.items() if v > 0})

    def barrier(self, pool=False):
        for e in ["pe", "act", "dve", "sp"] + (["pool"] if pool else []):
            self._wait(e, {k: v for k, v in self.cnt.items() if v > 0 and not k.startswith("d_pool")})


def build(depth=4):
    nc = bass.Bass("TRN2", target_bir_lowering=False)

    def din(name, shape):
        return nc.dram_tensor(name, list(shape), F32, kind="ExternalInput").ap()

    def dout(name, shape):
        return nc.dram_tensor(name, list(shape), F32, kind="ExternalOutput").ap()

    I = {}
    I["xp"] = din("xp", [NP, D])
    I["xs"] = din("xs", [NS, D])
    I["st_conv"] = din("st_conv", [NSEQ * 3, D])
    I["st_h"] = din("st_h", [NSEQ, D])
    I["st_re"] = din("st_re", [NSEQ, 4096])
    I["st_im"] = din("st_im", [NSEQ, 4096])
    for nm, shp in [("norm_mix", [4, D]), ("norm_ffn", [4, D]), ("norm_f", [D]),
                    ("w_ff1", [4, D, 4 * D]), ("w_ff2", [4, 4 * D, D]),
                    ("w_in_a", [2, D, 2 * D]), ("sgu_g", [2, D]), ("w_s", [2, 8, 128, 128]), ("b_s", [2, 8, 128]),
                    ("w_out_a", [2, D, D]), ("w_in_b", [1, D, 2 * D]), ("conv_w", [1, 4, D]), ("conv_b", [1, D]),
                    ("w_a", [1, 16, 64, 64]), ("b_a", [1, D]), ("w_x", [1, 16, 64, 64]), ("b_x", [1, D]),
                    ("lam", [1, D]), ("w_out_b", [1, D, D]), ("w_in_c", [1, D, D]),
                    ("lam_re", [1, 64, 64]), ("lam_im", [1, 64, 64]), ("log_dt", [1, 64]),
                    ("b_re", [1, 64, 64, 16]), ("b_im", [1, 64, 64, 16]),
                    ("c_re", [1, 64, 16, 64]), ("c_im", [1, 64, 16, 64]),
                    ("d_skip", [1, D]), ("w_glu", [1, D, 2 * D])]:
        if depth == 0 and nm.startswith("w_"):
            continue
        I[nm] = din(nm, shp)
    O = {}
    O["y_p"] = dout("y_p", [NP, D])
    O["y_s"] = dout("y_s", [NS, D])
    O["v_s"] = dout("v_s", [2, NS, D])
    O["conv_p"] = dout("conv_p", [3, D])
    O["h_p"] = dout("h_p", [D])
    O["conv_s"] = dout("conv_s", [NSEQ * 3, D])
    O["h_s"] = dout("h_s", [NSEQ, D])
    O["sre_p"] = dout("sre_p", [4096])
    O["sim_p"] = dout("sim_p", [4096])
    O["sre_s"] = dout("sre_s", [NSEQ, 4096])
    O["sim_s"] = dout("sim_s", [NSEQ, 4096])

    with ExitStack() as top:
        S = Sched(nc, top)

        tctr = [0]

        def T(st, name, shape, dt):
            tctr[0] += 1
            return st.enter_context(nc.sbuf_tensor("%s_%d" % (name, tctr[0]), list(shape), dt))

        x = T(top, "x", [128, 8, NT], F32)
        bx = S.bufs(5)
        W = T(top, "W", [128, 4, 8, 1024], BF16)
        bW = S.bufs(4)
        ps = top.enter_context(nc.psum_tensor("ps", [128, 8, 512], F32))
        psb = S.bufs(8)
        ident = T(top, "ident", [128, 128], F32); b_ident = S.buf()
        onesf = T(top, "onesf", [128, 128], F32); b_onesf = S.buf()
        onesb = T(top, "onesb", [128, 128], BF16); b_onesb = S.buf()
        tril = T(top, "tril", [128, 128], F32); b_tril = S.buf()
        cst = T(top, "cst", [128, 8], F32); b_cst = S.buf()
        gmix = T(top, "gmix", [128, 4, 8], F32)
        gffn = T(top, "gffn", [128, 4, 8], F32)
        gfin = T(top, "gfin", [128, 8], F32)
        b_g = S.buf()
        sq = T(top, "sq", [128, 8, 512], BF16); b_sq = S.buf()
        rs = T(top, "rs", [128, 512], F32); b_rs = S.buf()
        rs2 = T(top, "rs2", [128, 512], F32); b_rs2 = S.buf()

        bank_ctr = [0]

        def nb(n=1):
            if n == 2 and bank_ctr[0] % 2 == 1:
                bank_ctr[0] += 1
            b = bank_ctr[0] % 8
            bank_ctr[0] += n
            return b

        S.op("pool", lambda: nc.gpsimd.memset(onesf[:], 1.0), writes=[b_onesf])
        S.op("pool", lambda: nc.gpsimd.memset(onesb[:], 1.0), writes=[b_onesb])
        S.op("pool", lambda: nc.gpsimd.affine_select(ident[:], onesf[:], pattern=[[-1, 128]], compare_op=ALU.is_equal,
                                                      fill=0.0, base=0, channel_multiplier=1),
             reads=[b_onesf], writes=[b_ident])
        S.op("pool", lambda: nc.gpsimd.affine_select(tril[:], onesf[:], pattern=[[1, 128]], compare_op=ALU.is_ge,
                                                      fill=0.0, base=0, channel_multiplier=-1),
             reads=[b_onesf], writes=[b_tril])
        S.op("pool", lambda: nc.gpsimd.memset(cst[:, 0:1], EPS), writes=[b_cst])
        S.op("pool", lambda: nc.gpsimd.memset(cst[:, 1:2], 1.0), writes=[b_cst])
        S.op("pool", lambda: nc.gpsimd.memset(cst[:, 2:3], 0.0), writes=[b_cst])
        S.op("pool", lambda: nc.gpsimd.memset(cst[:, 3:4], -0.5), writes=[b_cst])
        S.op("pool", lambda: nc.gpsimd.memset(cst[:, 4:5], 0.5), writes=[b_cst])

        def small_dma(out, in_, writes, q="sp"):
            with nc.allow_non_contiguous_dma(reason="small param load"):
                return S.dma(q, out, in_, writes=writes)

        small_dma(gmix[:], I["norm_mix"].rearrange("l (c p) -> p l c", p=128), [b_g])
        small_dma(gffn[:], I["norm_ffn"].rearrange("l (c p) -> p l c", p=128), [b_g])
        small_dma(gfin[:], I["norm_f"].rearrange("(c p) -> p c", p=128), [b_g])
        s5p = {}
        if depth > 2:
            b_s5p = S.buf()
            for nm_ in ("lr", "li", "dtt"):
                s5p[nm_] = T(top, "s5p_" + nm_, [128, 4, 8], F32)
            s5p["dcol"] = T(top, "s5p_dcol", [128, 8], F32)

        wblocks = []
        for l in range(depth):
            kind = l % 3
            j = l // 3
            if kind == 0:
                wblocks += [I["w_in_a"][j, :, 0:1024], I["w_in_a"][j, :, 1024:2048], I["w_out_a"][j]]
            elif kind == 1:
                wblocks += [I["w_in_b"][j, :, 0:1024], I["w_in_b"][j, :, 1024:2048], I["w_out_b"][j]]
            else:
                wblocks += [I["w_in_c"][j], I["w_glu"][j, :, 0:1024], I["w_glu"][j, :, 1024:2048]]
            for q in range(4):
                wblocks += [I["w_ff1"][l, :, q * 1024:(q + 1) * 1024], I["w_ff2"][l, q * 1024:(q + 1) * 1024, :]]
        wnext = [0]

        def wissue(upto, after=()):
            lim = min(upto, len(wblocks) - 1)
            while wnext[0] <= lim:
                k = wnext[0]
                src = wblocks[k].rearrange("(kc p) f -> p kc f", p=128)
                for hh in range(2):
                    S.dma("pool", W[:, k % 4, 4 * hh:4 * hh + 4, :], src[:, 4 * hh:4 * hh + 4, :],
                          reads=list(after), writes=[bW[k % 4]])
                wnext[0] += 1

        def mm_acc(out_ap, pairs, reads, wbuf, **kw):
            n = len(pairs)
            for i, (l_, r_) in enumerate(pairs):
                S.op("pe", lambda: nc.tensor.matmul(out_ap, lhsT=l_, rhs=r_, start=(i == 0), stop=(i == n - 1), **kw),
                     reads=reads, writes=[wbuf], inc=(i == n - 1))

        def act(out, in_, func, reads, writes, **kw):
            return S.op("act", lambda: nc.scalar.activation(out=out, in_=in_, func=func, **kw), reads=reads, writes=writes)

        def tt(out, in0, in1, op, reads, writes, e="dve"):
            return S.op(e, lambda: S.eng[e].tensor_tensor(out, in0, in1, op), reads=reads, writes=writes)

        def ts(out, in0, s1, s2, op0, op1, reads, writes, e="dve"):
            if op1 is None:
                return S.op(e, lambda: S.eng[e].tensor_scalar(out, in0, s1, None, op0), reads=reads, writes=writes)
            return S.op(e, lambda: S.eng[e].tensor_scalar(out, in0, s1, s2, op0, op1), reads=reads, writes=writes)

        def stt(out, in0, scalar, in1, op0, op1, reads, writes):
            return S.op("dve", lambda: nc.vector.scalar_tensor_tensor(out, in0, scalar, in1, op0, op1),
                        reads=reads, writes=writes)

        def cp(out, in_, reads, writes, e="dve"):
            return S.op(e, lambda: S.eng[e].tensor_copy(out, in_), reads=reads, writes=writes)

        def transpose_to(bank, col0, in_ap, rows, cols, reads, inc=True):
            S.op("pe", lambda: nc.tensor.transpose(ps[:cols, bank, col0:col0 + rows], in_ap, ident[:rows, :rows]),
                 reads=list(reads) + [b_ident], writes=[psb[bank]], inc=inc)

        def norm_a(t):
            t0, n = TILES[t]
            act(sq[:, :, :n], x[:, :, t0:t0 + n], AF.Square, [bx[t]], [b_sq])

        def norm(t, g_ap, out_ap, out_buf, skip_a=False):
            t0, n = TILES[t]
            if not skip_a:
                norm_a(t)
            bk = nb()
            mm_acc(ps[:, bk, :n], [(onesb[:, :], sq[:, c, :n]) for c in range(8)], [b_sq, b_onesb], psb[bk])
            act(rs[:, :n], ps[:, bk, :n], AF.Sqrt, [psb[bk], b_cst], [b_rs], scale=1.0 / D, bias=cst[:, 0:1])
            S.op("dve", lambda: nc.vector.reciprocal(rs2[:, :n], rs[:, :n]), reads=[b_rs], writes=[b_rs2])
            for c in range(8):
                stt(out_ap[:, c, :n], x[:, c, t0:t0 + n], g_ap[:, c:c + 1], rs2[:, :n], ALU.mult, ALU.mult,
                    [bx[t], b_rs2, b_g], [out_buf])

        def resid_add(t, oc, bk):
            t0, n = TILES[t]
            tt(x[:, oc, t0:t0 + n], ps[:, bk, :n], x[:, oc, t0:t0 + n], ALU.add, [psb[bk], bx[t]], [bx[t]])

        def out_proj(t, Wo, bWo, ybuf, b_y):
            t0, n = TILES[t]
            for oc in range(8):
                bk = nb()
                mm_acc(ps[:, bk, :n], [(Wo[:, fc, oc * 128:(oc + 1) * 128], ybuf[:, fc, :n]) for fc in range(8)],
                       [bWo, b_y], psb[bk])
                resid_add(t, oc, bk)

        with ExitStack() as st:
            xin = [T(st, "xin%d" % i, [128, 1024], F32) for i in range(5)]
            b_xin = S.bufs(5)
            wst = T(st, "wst", [128, 8, 1024], F32); b_wst = S.buf()

            def fast_block(k):
                src = wblocks[k].rearrange("(kc p) f -> p kc f", p=128)
                for hh in range(2):
                    S.dma("sp", wst[:, 4 * hh:4 * hh + 4, :], src[:, 4 * hh:4 * hh + 4, :], writes=[b_wst])
                for hh in range(2):
                    cp(W[:, k % 4, 4 * hh:4 * hh + 4, :], wst[:, 4 * hh:4 * hh + 4, :], [b_wst], [bW[k % 4]])
                wnext[0] = max(wnext[0], k + 1)
            if depth > 0:
                fast_block(0)
            for blk in range(17):
                rows = 128 if blk < 16 else 64
                src = I["xp"][blk * 128:(blk + 1) * 128, :] if blk < 16 else I["xs"][:, :]
                t = blk // 4 if blk < 16 else 4
                tok0 = blk * 128
                xi, bxi = xin[blk % 5], b_xin[blk % 5]
                S.dma("sp", xi[:rows, :], src, writes=[bxi])
                for hh in range(2):
                    bk = nb()
                    for j_ in range(4):
                        c = 4 * hh + j_
                        transpose_to(bk, j_ * 128, xi[:rows, c * 128:(c + 1) * 128], rows, 128, [bxi], inc=(j_ == 3))
                    pv_ = ps[:, bk, :].rearrange("p (j t) -> p j t", j=4)[:, :, :rows]
                    if hh == 0:
                        act(x[:, 4 * hh:4 * hh + 4, tok0:tok0 + rows], pv_, AF.Copy, [psb[bk]], [bx[t]])
                    else:
                        cp(x[:, 4 * hh:4 * hh + 4, tok0:tok0 + rows], pv_, [psb[bk]], [bx[t]])
            if depth > 0:
                fast_block(1)
            S.barrier()

        def s5_param_prefetch():
            if depth <= 2:
                return
            pv_ = "(c kk g2) p -> (g2 p) kk c"
            for kk in range(4):
                small_dma(s5p["lr"][:, kk, :], I["lam_re"][0].rearrange(pv_, kk=4, g2=2)[:, kk, :], [b_s5p])
                small_dma(s5p["li"][:, kk, :], I["lam_im"][0].rearrange(pv_, kk=4, g2=2)[:, kk, :], [b_s5p])
                for g2 in range(2):
                    small_dma(s5p["dtt"][64 * g2:64 * g2 + 64, kk, :],
                              I["log_dt"][0].rearrange("(c kk g2) -> g2 kk c", kk=4, g2=2)[g2, kk].partition_broadcast(64), [b_s5p])
            small_dma(s5p["dcol"][:], I["d_skip"][0].rearrange("(c p) -> p c", p=128), [b_s5p])


        def store_rows(dst, tile_ap, reads):
            S.dma("sp", dst, tile_ap, reads=reads)

        def ffn(l, wb0):
            with ExitStack() as st:
                xn = T(st, "xn_all", [128, 8, NT], BF16)
                b_xn = S.bufs(5)
                hb = [T(st, "hb%d" % i, [128, 8, 512], BF16) for i in range(2)]
                b_hb = S.bufs(2)
                rl = [T(st, "rl%d" % i, [128, 512], F32) for i in range(2)]
                b_rl = S.bufs(2)
                norm(0, gffn[:, l, :], xn[:, :, TILES[0][0]:TILES[0][0] + TILES[0][1]], b_xn[0])
                rlc = [0]
                for q in range(4):
                    i1 = wb0 + 2 * q
                    i2 = i1 + 1
                    wissue(i2 + 2)
                    W1, W2 = W[:, i1 % 4], W[:, i2 % 4]
                    bW1, bW2 = bW[i1 % 4], bW[i2 % 4]

                    def ff1(t):
                        t0, n = TILES[t]
                        h_, bh_ = hb[t % 2], b_hb[t % 2]
                        if q == 0 and t + 1 < 5:
                            norm_a(t + 1)
                        for fc in range(8):
                            bk = nb()
                            mm_acc(ps[:, bk, :n],
                                   [(W1[:, kc, fc * 128:(fc + 1) * 128], xn[:, kc, t0:t0 + n]) for kc in range(8)],
                                   [bW1, b_xn[t]], psb[bk])
                            ri = rlc[0] % 2
                            rlc[0] += 1
                            act(rl[ri][:, :n], ps[:, bk, :n], AF.Relu, [psb[bk]], [b_rl[ri]])
                            act(h_[:, fc, :n], rl[ri][:, :n], AF.Square, [b_rl[ri]], [bh_])
                        if q == 0 and t + 1 < 5:
                            t1_, n1_ = TILES[t + 1]
                            norm(t + 1, gffn[:, l, :], xn[:, :, t1_:t1_ + n1_], b_xn[t + 1], skip_a=True)

                    def ff2(t):
                        t0, n = TILES[t]
                        h_, bh_ = hb[t % 2], b_hb[t % 2]
                        for oc in range(8):
                            bk = nb()
                            mm_acc(ps[:, bk, :n],
                                   [(W2[:, fc, oc * 128:(oc + 1) * 128], h_[:, fc, :n]) for fc in range(8)],
                                   [bW2, bh_], psb[bk])
                            resid_add(t, oc, bk)

                    ff1(0)
                    for t in range(5):
                        if t + 1 < 5:
                            ff1(t + 1)
                        ff2(t)
                S.barrier()

        def sgu(l, j, wb0):
            with ExitStack() as st:
                sggB = T(st, "sggB", [128, 1024], F32)
                bsB = T(st, "bsB", [128, 8, 128], F32)
                wsn = T(st, "wsn", [128, 8, 128], F32)
                WmT = T(st, "WmT", [128, 8, 128], BF16)
                wS32 = T(st, "wS32", [64, 8, 64], F32)
                WmS = T(st, "WmS", [64, 8, 64], BF16)
                b_c = S.buf(); b_wsn = S.buf(); b_WmT = S.buf(); b_wS = S.buf(); b_WmS = S.buf()
                xnT = [T(st, "xnT0", [128, 8, 512], BF16)] * 2
                b_xnT = [S.buf()] * 2
                u = T(st, "u", [128, 8, 512], F32); b_u = S.buf()
                vg = [T(st, "vg0", [128, 1024], F32)] * 2; b_vg = [S.buf()] * 2
                vb = [T(st, "vb%d" % i, [128, 1024], BF16) for i in range(2)]; b_vb = S.bufs(2)
                ss = T(st, "ss", [128, 4], F32); b_ss = S.buf()
                ssr = [T(st, "ssr%d" % i, [128, 1], F32) for i in range(2)]; b_ssr = S.bufs(2)
                tmp = T(st, "tmp", [128, 8, 128], F32); b_tmp = S.buf()
                ybuf = [T(st, "ybuf0", [128, 8, 512], BF16)] * 2; b_y = [S.buf()] * 2
                if l != 0:
                    wissue(wb0 + 3)
                Wu, Wv, Wo = W[:, wb0 % 4], W[:, (wb0 + 1) % 4], W[:, (wb0 + 2) % 4]
                bWu, bWv, bWo = bW[wb0 % 4], bW[(wb0 + 1) % 4], bW[(wb0 + 2) % 4]
                norm(0, gmix[:, l, :], xnT[0], b_xnT[0])
                for fc in range(8):
                    bk = nb()
                    mm_acc(ps[:, bk, :512], [(Wu[:, kc, fc * 128:(fc + 1) * 128], xnT[0][:, kc, :512]) for kc in range(8)],
                           [bWu, b_xnT[0]], psb[bk])
                    act(u[:, fc, :512], ps[:, bk, :512], AF.Gelu, [psb[bk]], [b_u])
                small_dma(sggB[:], I["sgu_g"][j].partition_broadcast(128), [b_c])
                small_dma(bsB[:].rearrange("p g t -> p (g t)"),
                          I["b_s"][j].rearrange("g t -> (g t)").partition_broadcast(128), [b_c])
                S.dma("sp", wsn[:], I["w_s"][j].rearrange("g t s -> t g s"), writes=[b_wsn])
                for hh in range(2):
                    bk = nb()
                    for j_ in range(4):
                        transpose_to(bk, j_ * 128, wsn[:, 4 * hh + j_, :], 128, 128, [b_wsn], inc=(j_ == 3))
                    tt(WmT[:, 4 * hh:4 * hh + 4, :], ps[:, bk, :].rearrange("p (j t) -> p j t", j=4),
                       tril[:, :].unsqueeze(1).to_broadcast([128, 4, 128]), ALU.mult, [psb[bk], b_tril], [b_WmT])
                S.op("dve", lambda: nc.vector.memset(wS32[:], 0.0), writes=[b_wS])
                w4t = T(st, "w4t", [4, 32], F32); b_w4 = S.buf()
                w4 = w4t[:, :].rearrange("p (g t) -> p g t", g=8)
                for g_ in range(8):
                    small_dma(w4[:, g_, :], I["w_s"][j, g_, 0:4, 0:4].rearrange("t s -> s t"), [b_w4])
                for b_ in range(NSEQ):
                    S.dma("sp", wS32[4 * b_:4 * b_ + 4, :, 4 * b_:4 * b_ + 4], w4, reads=[b_w4], writes=[b_wS])
                tt(WmS[:], wS32[:], tril[:64, :64].unsqueeze(1).to_broadcast([64, 8, 64]), ALU.mult,
                   [b_wS, b_tril], [b_WmS])
                if l == 0:
                    s5_param_prefetch()

                Wu, Wv, Wo = W[:, wb0 % 4], W[:, (wb0 + 1) % 4], W[:, (wb0 + 2) % 4]
                bWu, bWv, bWo = bW[wb0 % 4], bW[(wb0 + 1) % 4], bW[(wb0 + 2) % 4]
                vg[1] = wsn[:].rearrange("p g t -> p (g t)")
                b_vg[1] = b_wsn
                chc = [0]

                def stageA(t, ch, k_):
                    t0, n = TILES[t]
                    xn_, bxn_ = xnT[0], b_xnT[0]
                    cn = min(n, 128)
                    c0 = ch * 128
                    for hf in range(2):
                        bk = nb()
                        mm_acc(ps[:cn, bk, :],
                               [(xn_[:, kc, c0:c0 + cn], Wv[:, kc, hf * 512:(hf + 1) * 512]) for kc in range(8)],
                               [bWv, bxn_], psb[bk])
                        act(vg[k_][:cn, hf * 512:(hf + 1) * 512], ps[:cn, bk, :], AF.Gelu, [psb[bk]], [b_vg[k_]])
                    tmpf = tmp[:].rearrange("p g t -> p (g t)")
                    act(tmpf[:cn, :], vg[k_][:cn, :], AF.Square, [b_vg[k_]], [b_tmp])
                    S.op("dve", lambda: nc.vector.reduce_sum(ss[:cn, 0:1], tmpf[:cn, :], mybir.AxisListType.X),
                         reads=[b_tmp], writes=[b_ss])
                    ts(ss[:cn, 1:2], ss[:cn, 0:1], 1.0 / D, EPS, ALU.mult, ALU.add, [b_ss], [b_ss])
                    tt(ssr[k_][:cn, 0:1], ss[:cn, 1:2], cst[:cn, 3:4], ALU.pow, [b_ss, b_cst], [b_ssr[k_]], e="pool")

                def stageA2(t, ch, k_):
                    t0, n = TILES[t]
                    cn = min(n, 128)
                    stt(vb[k_][:cn, :], vg[k_][:cn, :], ssr[k_][:cn, 0:1], sggB[:cn, :], ALU.mult, ALU.mult,
                        [b_vg[k_], b_ssr[k_], b_c], [b_vb[k_]])
                    if t == 4:
                        vf_, bvf_ = vg[1 - k_], b_vg[1 - k_]
                        stt(vf_[:cn, :], vg[k_][:cn, :], ssr[k_][:cn, 0:1], sggB[:cn, :], ALU.mult, ALU.mult,
                            [b_vg[k_], b_ssr[k_], b_c], [bvf_])
                        S.dma("sp", O["v_s"][j], vf_[:cn, :], reads=[bvf_])

                def stageB(t, ch, k_):
                    t0, n = TILES[t]
                    yb_, by_ = ybuf[0], b_y[0]
                    cn = min(n, 128)
                    c0 = ch * 128
                    bk = nb(2)
                    pm = ps[:, bk:bk + 2, :].rearrange("p b f -> p (b f)")
                    for g in range(8):
                        rhs = WmT[:cn, g, :cn] if t < 4 else WmS[:cn, g, :cn]
                        bkk = bk + (g * cn) // 512
                        S.op("pe", lambda: nc.tensor.matmul(pm[:, g * cn:(g + 1) * cn],
                                                            lhsT=vb[k_][:cn, g * 128:(g + 1) * 128], rhs=rhs,
                                                            start=True, stop=True),
                             reads=[b_vb[k_], b_WmT, b_WmS], writes=[psb[bkk]],
                             inc=(g == 7 or (cn == 128 and g == 3)))
                    pmv = pm[:, :8 * cn].rearrange("p (g t) -> p g t", g=8)
                    rb = [psb[bk]] + ([psb[bk + 1]] if cn == 128 else [])
                    if t < 4:
                        tt(tmp[:, :, :cn], pmv, bsB[:, :, :cn], ALU.add, rb + [b_c], [b_tmp])
                    else:
                        tt(tmp[:, :, :cn].rearrange("p g (b t) -> p g b t", t=4),
                           pmv.rearrange("p g (b t) -> p g b t", t=4),
                           bsB[:, :, 0:4].unsqueeze(2).to_broadcast([128, 8, NSEQ, 4]), ALU.add, rb + [b_c], [b_tmp])
                    tt(yb_[:, :, c0:c0 + cn], tmp[:, :, :cn], u[:, :, c0:c0 + cn], ALU.mult, [b_tmp, b_u], [by_])

                def tile_norm(t):
                    norm(t, gmix[:, l, :], xnT[0], b_xnT[0])

                def tile_u(t):
                    t0, n = TILES[t]
                    for fc in range(8):
                        bk = nb()
                        mm_acc(ps[:, bk, :n], [(Wu[:, kc, fc * 128:(fc + 1) * 128], xnT[0][:, kc, :n]) for kc in range(8)],
                               [bWu, b_xnT[0]], psb[bk])
                        act(u[:, fc, :n], ps[:, bk, :n], AF.Gelu, [psb[bk]], [b_u])

                wissue(wb0 + 3)
                for t in range(5):
                    t0, n = TILES[t]
                    nch = max(1, n // 128)
                    ks = []
                    for ch in range(nch):
                        ks.append(chc[0] % 2)
                        chc[0] += 1
                    stageA(t, 0, ks[0])
                    stageA2(t, 0, ks[0])
                    for ch in range(nch):
                        if ch + 1 < nch:
                            stageA(t, ch + 1, ks[ch + 1])
                        elif t + 1 < 5:
                            tile_norm(t + 1)
                        stageB(t, ch, ks[ch])
                        if ch + 1 < nch:
                            stageA2(t, ch + 1, ks[ch + 1])
                    if t + 1 < 5:
                        tile_u(t + 1)
                    out_proj(t, Wo, bWo, ybuf[0], b_y[0])
                wissue(wb0 + 6)
                S.barrier(pool=True)

        def rglru(l, j, wb0):
            with ExitStack() as st:
                convw = T(st, "convw", [128, 8, 4], F32)
                pc = T(st, "pc", [128, 8, 8], F32)
                WaBD = T(st, "WaBD", [128, 8, 128], BF16)
                WxBD = T(st, "WxBD", [128, 8, 128], BF16)
                b_c = S.buf(); b_wa = S.buf()
                halo = T(st, "halo", [128, 8, 3], F32); b_halo = S.buf()
                hcar = T(st, "hcar", [128, 8], F32); b_hcar = S.buf()
                halo_s = T(st, "halo_s", [128, 8, NSEQ * 3], F32); b_halo_s = S.buf()
                h0s = T(st, "h0s", [128, 8, NSEQ], F32); b_h0s = S.buf()
                convs_o = T(st, "convs_o", [128, 8, NSEQ * 3], F32); b_convs_o = S.buf()
                hs_o = T(st, "hs_o", [128, 8, NSEQ], F32); b_hs_o = S.buf()
                stin = T(st, "stin", [64, 1024], F32); b_stin = S.buf()
                stout = stin; b_stout = b_stin
                xnT = [T(st, "xnTb0", [128, 8, 512], BF16)] * 2; b_xnT = [S.buf()] * 2
                ybuf = [T(st, "ybufb0", [128, 8, 512], BF16)] * 2; b_y = [S.buf()] * 2
                NSET = 2
                names = ["gate", "xc", "r", "ig", "a", "a2"]
                sets = []
                for i in range(NSET):
                    d_ = {nm: T(st, "%s%d" % (nm, i), [128, 512], F32) for nm in names}
                    d_["xext"] = T(st, "xext%d" % i, [128, 520], F32)
                    d_["xcb"] = T(st, "xcb%d" % i, [128, 512], BF16)
                    d_["b"] = {nm: S.buf() for nm in names + ["xext", "xcb"]}
                    d_["hc"] = d_["r"]
                    d_["b"]["hc"] = d_["b"]["r"]
                    sets.append(d_)
                wissue(wb0 + 3)
                for k in range(4):
                    small_dma(convw[:, :, k], I["conv_w"][j, k].rearrange("(c p) -> p c", p=128), [b_c])
                for idx, nm in enumerate(["conv_b", "b_a", "b_x", "lam"]):
                    small_dma(pc[:, :, idx], I[nm][j].rearrange("(c p) -> p c", p=128), [b_c])
                act(pc[:, :, 6], pc[:, :, 3], AF.Exp, [b_c], [b_c], scale=-1.0)
                act(pc[:, :, 4], pc[:, :, 6], AF.Ln, [b_c, b_cst], [b_c], bias=cst[:, 1:2])
                ts(pc[:, :, 5], pc[:, :, 4], -4.0, None, ALU.mult, None, [b_c], [b_c])
                ts(pc[:, :, 4], pc[:, :, 4], -8.0, None, ALU.mult, None, [b_c], [b_c])
                ts(pc[:, :, 1], pc[:, :, 1], 0.5, None, ALU.mult, None, [b_c], [b_c])
                ts(pc[:, :, 2], pc[:, :, 2], 0.5, None, ALU.mult, None, [b_c], [b_c])
                S.op("dve", lambda: nc.vector.memset(WaBD[:], 0.0), writes=[b_wa])
                S.op("dve", lambda: nc.vector.memset(WxBD[:], 0.0), writes=[b_wa])
                for (wt, nm) in [(WaBD, "w_a"), (WxBD, "w_x")]:
                    srcv = I[nm][j].rearrange("(c h2) i j -> h2 i c j", h2=2)
                    for h2 in range(2):
                        with nc.allow_non_contiguous_dma(reason="blockdiag"):
                            S.dma("pool", wt[64 * h2:64 * h2 + 64, :, 64 * h2:64 * h2 + 64], srcv[h2], writes=[b_wa])
                S.op("dve", lambda: nc.vector.memset(halo[:], 0.0), writes=[b_halo])
                S.op("dve", lambda: nc.vector.memset(hcar[:], 0.0), writes=[b_hcar])
                S.dma("sp", stin[:48, :], I["st_conv"][:, :], writes=[b_stin])
                for hh in range(2):
                    bk = nb()
                    for j_ in range(4):
                        c = 4 * hh + j_
                        transpose_to(bk, j_ * 128, stin[:48, c * 128:(c + 1) * 128], 48, 128, [b_stin], inc=(j_ == 3))
                    cp(halo_s[:, 4 * hh:4 * hh + 4, :], ps[:, bk, :].rearrange("p (j t) -> p j t", j=4)[:, :, :48],
                       [psb[bk]], [b_halo_s])
                S.dma("sp", stin[:16, :], I["st_h"][:, :], writes=[b_stin])
                for hh in range(2):
                    bk = nb()
                    for j_ in range(4):
                        c = 4 * hh + j_
                        transpose_to(bk, j_ * 128, stin[:16, c * 128:(c + 1) * 128], 16, 128, [b_stin], inc=(j_ == 3))
                    cp(h0s[:, 4 * hh:4 * hh + 4, :], ps[:, bk, :].rearrange("p (j t) -> p j t", j=4)[:, :, :16],
                       [psb[bk]], [b_h0s])

                Wg, Wx_, Wo = W[:, wb0 % 4], W[:, (wb0 + 1) % 4], W[:, (wb0 + 2) % 4]
                bWg, bWx, bWo = bW[wb0 % 4], bW[(wb0 + 1) % 4], bW[(wb0 + 2) % 4]
                def P1(t, c, Z):
                    t0, n = TILES[t]
                    smp = (t == 4)
                    xn_, bxn_ = xnT[0], b_xnT[0]
                    B = Z["b"]
                    bk = nb()
                    mm_acc(ps[:, bk, :n], [(Wg[:, kc, c * 128:(c + 1) * 128], xn_[:, kc, :n]) for kc in range(8)],
                           [bWg, bxn_], psb[bk])
                    act(Z["gate"][:, :n], ps[:, bk, :n], AF.Gelu, [psb[bk]], [B["gate"]])
                    bk = nb()
                    mm_acc(ps[:, bk, :n], [(Wx_[:, kc, c * 128:(c + 1) * 128], xn_[:, kc, :n]) for kc in range(8)],
                           [bWx, bxn_], psb[bk])
                    if not smp:
                        cp(Z["xext"][:, 3:3 + n], ps[:, bk, :n], [psb[bk]], [B["xext"]])
                        cp(Z["xext"][:, 0:3], halo[:, c, :], [b_halo], [B["xext"]])
                        cp(halo[:, c, :], Z["xext"][:, n:n + 3], [B["xext"]], [b_halo])

                        def xk(k):
                            return Z["xext"][:, k:k + n]
                        xcv = Z["xc"][:, :n]
                    else:
                        xe3 = Z["xext"][:, 0:NSEQ * 7].rearrange("p (b k) -> p b k", k=7)
                        cp(xe3[:, :, 3:7], ps[:, bk, :n].rearrange("p (b k) -> p b k", k=4),
                           [psb[bk]], [B["xext"]])
                        cp(xe3[:, :, 0:3], halo_s[:, c, :].rearrange("p (b k) -> p b k", k=3), [b_halo_s], [B["xext"]])
                        cp(convs_o[:, c, :].rearrange("p (b k) -> p b k", k=3), xe3[:, :, 4:7], [B["xext"]], [b_convs_o])

                        def xk(k):
                            return xe3[:, :, k:k + 4]
                        xcv = Z["xc"][:, :n].rearrange("p (b k) -> p b k", k=4)
                    ts(xcv, xk(0), convw[:, c, 0:1], pc[:, c, 0:1], ALU.mult, ALU.add, [B["xext"], b_c], [B["xc"]])
                    for k in range(1, 4):
                        stt(xcv, xk(k), convw[:, c, k:k + 1], xcv, ALU.mult, ALU.add, [B["xext"], b_c, B["xc"]], [B["xc"]])
                    cp(Z["xcb"][:, :n], Z["xc"][:, :n], [B["xc"]], [B["xcb"]])

                def P2(t, c, Z):
                    t0, n = TILES[t]
                    smp = (t == 4)
                    B = Z["b"]
                    yb_, by_ = ybuf[0], b_y[0]
                    bk = nb()
                    mm_acc(ps[:, bk, :n], [(WaBD[:, c, :], Z["xcb"][:, :n])], [b_wa, B["xcb"]], psb[bk])
                    act(Z["r"][:, :n], ps[:, bk, :n], AF.Tanh, [psb[bk], b_c], [B["r"]], scale=0.5, bias=pc[:, c, 1:2])
                    bk = nb()
                    mm_acc(ps[:, bk, :n], [(WxBD[:, c, :], Z["xcb"][:, :n])], [b_wa, B["xcb"]], psb[bk])
                    act(Z["ig"][:, :n], ps[:, bk, :n], AF.Tanh, [psb[bk], b_c], [B["ig"]], scale=0.5, bias=pc[:, c, 2:3])
                    act(Z["a"][:, :n], Z["r"][:, :n], AF.Exp, [B["r"], b_c], [B["a"]], scale=pc[:, c, 5:6], bias=pc[:, c, 5:6])
                    act(Z["a2"][:, :n], Z["r"][:, :n], AF.Exp, [B["r"], b_c], [B["a2"]], scale=pc[:, c, 4:5], bias=pc[:, c, 4:5])
                    act(Z["a2"][:, :n], Z["a2"][:, :n], AF.Sqrt, [B["a2"], b_cst], [B["a2"]], scale=-1.0, bias=cst[:, 1:2])
                    stt(Z["ig"][:, :n], Z["ig"][:, :n], 1.0, Z["xc"][:, :n], ALU.add, ALU.mult, [B["ig"], B["xc"]], [B["ig"]])
                    stt(Z["ig"][:, :n], Z["ig"][:, :n], 0.5, Z["a2"][:, :n], ALU.mult, ALU.mult, [B["ig"], B["a2"]], [B["ig"]])
                    if not smp:
                        S.op("dve", lambda: nc.vector.tensor_tensor_scan(Z["hc"][:, :n], Z["a"][:, :n], Z["ig"][:, :n],
                                                                         hcar[:, c:c + 1], ALU.mult, ALU.add),
                             reads=[B["a"], B["ig"], b_hcar], writes=[B["hc"]])
                        cp(hcar[:, c:c + 1], Z["hc"][:, n - 1:n], [B["hc"]], [b_hcar])
                    else:
                        a3 = Z["a"][:, :n].rearrange("p (b k) -> p b k", k=4)
                        g3 = Z["ig"][:, :n].rearrange("p (b k) -> p b k", k=4)
                        r3 = Z["r"][:, 0:NSEQ].unsqueeze(2)
                        tt(r3, a3[:, :, 0:1], h0s[:, c, :].unsqueeze(2), ALU.mult, [B["a"], b_h0s, B["r"]], [B["r"]])
                        tt(g3[:, :, 0:1], g3[:, :, 0:1], r3, ALU.add, [B["ig"], B["r"]], [B["ig"]])
                        S.op("dve", lambda: nc.vector.memset(a3[:, :, 0:1], 0.0), reads=[B["r"]], writes=[B["a"]])
                        S.op("dve", lambda: nc.vector.tensor_tensor_scan(Z["hc"][:, :n], Z["a"][:, :n], Z["ig"][:, :n],
                                                                         0.0, ALU.mult, ALU.add),
                             reads=[B["a"], B["ig"]], writes=[B["hc"]])
                        cp(hs_o[:, c, :].unsqueeze(2), Z["hc"][:, :n].rearrange("p (b k) -> p b k", k=4)[:, :, 3:4],
                           [B["hc"]], [b_hs_o])
                    tt(yb_[:, c, :n], Z["hc"][:, :n], Z["gate"][:, :n], ALU.mult, [B["hc"], B["gate"]], [by_])

                def rg_state_outputs():
                    with nc.allow_non_contiguous_dma(reason="small state out"):
                        for k in range(3):
                            S.dma("sp", O["conv_p"][k].rearrange("(c p) -> p c", p=128), halo[:, :, k], reads=[b_halo])
                        S.dma("sp", O["h_p"].rearrange("(c p) -> p c", p=128), hcar[:], reads=[b_hcar])
                    for (src_t, bsrc, rows, dst) in [(convs_o, b_convs_o, 48, O["conv_s"]), (hs_o, b_hs_o, 16, O["h_s"])]:
                        for hh in range(2):
                            bk = nb()
                            for j_ in range(4):
                                c = 4 * hh + j_
                                transpose_to(bk, j_ * 128, src_t[:, c, :], 128, rows, [bsrc], inc=(j_ == 3))
                            cp(stout[:rows, 512 * hh:512 * hh + 512], ps[:rows, bk, :], [psb[bk]], [b_stout])
                        S.dma("sp", dst, stout[:rows, :], reads=[b_stout])

                jobs = [(t, c) for t in range(5) for c in range(8)]
                norm(0, gmix[:, l, :], xnT[0], b_xnT[0])
                P1(0, 0, sets[0])
                for n_, (t, c) in enumerate(jobs):
                    nxt = jobs[n_ + 1] if n_ + 1 < len(jobs) else None
                    if nxt is not None and nxt[1] != 0:
                        P1(nxt[0], nxt[1], sets[(n_ + 1) % NSET])
                    if nxt is not None and nxt[1] == 0:
                        norm(nxt[0], gmix[:, l, :], xnT[0], b_xnT[0])
                    P2(t, c, sets[n_ % NSET])
                    if c == 7:
                        if nxt is not None:
                            P1(nxt[0], nxt[1], sets[(n_ + 1) % NSET])
                        else:
                            rg_state_outputs()
                        out_proj(t, Wo, bWo, ybuf[0], b_y[0])
                wissue(wb0 + 6)
                S.barrier()

        def s5(l, j, wb0):
            ELL = 4
            L = 32
            TS5 = [(i * 128, 128) for i in range(16)] + [(2048, 64)]
            fslot = (wb0 + 3) % 4
            with ExitStack() as st:
                def PT(name, shape=(128, 4, 8), dt=F32):
                    return T(st, name, list(shape), dt)
                lr = s5p["lr"]; li = s5p["li"]; dtt = s5p["dtt"]; th = PT("th"); rho = PT("rho")
                c1 = PT("c1"); s1 = PT("s1"); qre = PT("qre"); qim = PT("qim")
                t1 = PT("t1"); t2 = PT("t2"); ti = PT("ti", dt=I32)
                pw = T(st, "pw", [128, ELL + 1, 2, 32], F32)
                Xu = [T(st, "Xu%d" % i, [128, 32], F32) for i in range(2)]
                Xs = [T(st, "Xs%d" % i, [128, 32], F32) for i in range(2)]
                dcol = s5p["dcol"]
                id32 = T(st, "id32", [128, 32], F32)
                b_p = b_s5p; b_Xc = S.buf(); b_dD = S.buf()
                cosT = T(st, "cosT", [128, 32, L], F32); sinT = T(st, "sinT", [128, 32, L], F32)
                rhoT = T(st, "rhoT", [128, 32, L], F32)
                b_cos = S.buf(); b_sin = S.buf(); b_rho = S.buf()
                TB = [b_cos, b_sin, b_rho]
                tA = T(st, "tA", [128, 32, L], F32); tB = T(st, "tB", [128, 32, L], F32); b_tA = S.buf(); b_tB = S.buf()
                wre = T(st, "wre", [128, 32, L], F32); wim = T(st, "wim", [128, 32, L], F32); b_wre = S.buf(); b_wim = S.buf()
                CT = T(st, "CT", [128, 2 * ELL, 8, 128], BF16); b_CT = S.buf()
                KT = T(st, "KT", [128, ELL, 8, 32], BF16); b_KT = S.buf()
                BBTv = W[:, fslot].rearrange("p kc f -> p (kc f)")
                b_BBT = bW[fslot]

                def BBT(i, ri):
                    o = (2 * i + ri) * 1024
                    return BBTv[:, o:o + 1024].rearrange("p (c m) -> p c m", c=8)
                ubfL = [T(st, "ubf%d" % i, [128, 8, 128], BF16) for i in range(2)]; b_ubfL = S.bufs(2)
                sp2 = T(st, "sp2", [128, 8, 256], BF16); b_sp2 = S.buf()
                spL = [sq[:, :, 256:512], sp2[:, :, :]]; b_spL = [b_sq, b_sp2]
                xnT = T(st, "xnTc", [128, 8, 128], BF16); b_xnT = S.buf()
                gb = T(st, "gb5", [128, 8, 128], BF16); b_gb = S.buf()
                bxs = S.bufs(len(TS5))
                glu_a = rs2; b_glu = b_rs2
                flat = lambda tl: tl[:].rearrange("p k t -> p (k t)")
                X0 = [flat(cosT)[:, 0:512].rearrange("p (k b) -> p k b", b=NSEQ),
                      flat(sinT)[:, 0:512].rearrange("p (k b) -> p k b", b=NSEQ)]
                b_X0L = [b_cos, b_sin]
                c8 = lambda ap: ap.rearrange("p (c f) -> p c f", c=8)
                ZBre, b_ZBre = c8(flat(wre)), b_wre
                ZBim, b_ZBim = c8(flat(wim)), b_wim
                CTf = [c8(flat(tA)), c8(flat(tB))]; b_CTf = [b_tA, b_tB]
                Ere, b_Ere = c8(flat(cosT)), b_cos
                Eim, b_Eim = c8(flat(sinT)), b_sin
                rhoTf = flat(rhoT)
                bbr = rhoTf[:, 0:512].rearrange("p (kk c h) -> p kk c h", kk=4, c=8)
                bbi = rhoTf[:, 512:1024].rearrange("p (kk c h) -> p kk c h", kk=4, c=8)
                b_bb = b_rho
                sin_t = flat(tA)[0:16, :]; b_sin_t = b_tA
                wissue(wb0 + 2)
                pv = "(c kk g2) p -> (g2 p) kk c"
                for kk in range(4):
                    with nc.allow_non_contiguous_dma(reason="b load"):
                        S.dma("sp", bbr[:, kk], I["b_re"][j].rearrange("(c kk g2) p h -> (g2 p) kk c h", kk=4, g2=2)[:, kk], writes=[b_bb])
                        S.dma("sp", bbi[:, kk], I["b_im"][j].rearrange("(c kk g2) p h -> (g2 p) kk c h", kk=4, g2=2)[:, kk], writes=[b_bb])
                for ri, (nm, E_, bE_) in enumerate([("c_re", Ere, b_Ere), ("c_im", Eim, b_Eim)]):
                    S.op("dve", lambda: nc.vector.memset(E_, 0.0), writes=[bE_])
                    srcv = I[nm][j].rearrange("(c kk g2) h p -> kk g2 h c p", kk=4, g2=2)
                    for kk in range(4):
                        for g2 in range(2):
                            S.dma("sp", E_[32 * kk + 16 * g2:32 * kk + 16 * g2 + 16, :, 64 * g2:64 * g2 + 64],
                                  srcv[kk, g2], writes=[bE_])
                P_ = [b_p]
                act(dtt[:], dtt[:], AF.Exp, P_, P_)
                tt(th[:], li[:], dtt[:], ALU.mult, P_, P_)
                tt(t1[:], lr[:], dtt[:], ALU.mult, P_, P_)
                act(rho[:], t1[:], AF.Exp, P_, P_)

                def sin_of(out, ang, shift):
                    ts(t1[:], ang, 1.0 / (2 * np.pi), shift, ALU.mult, ALU.add, P_, P_)
                    cp(ti[:], t1[:], P_, P_)
                    cp(t2[:], ti[:], P_, P_)
                    tt(t1[:], t1[:], t2[:], ALU.subtract, P_, P_)
                    ts(t2[:], t1[:], 0.5, None, ALU.is_gt, None, P_, P_)
                    tt(t1[:], t1[:], t2[:], ALU.subtract, P_, P_)
                    ts(t2[:], t1[:], -0.5, None, ALU.is_lt, None, P_, P_)
                    tt(t1[:], t1[:], t2[:], ALU.add, P_, P_)
                    act(out, t1[:], AF.Sin, P_, P_, scale=6.283185)
                sin_of(s1[:], th[:], 0.0)
                sin_of(c1[:], th[:], 0.25)
                f2 = "p kk c -> p (kk c)"
                S.op("dve", lambda: nc.vector.memset(pw[:, 0, 0, :], 1.0), writes=P_)
                S.op("dve", lambda: nc.vector.memset(pw[:, 0, 1, :], 0.0), writes=P_)
                tt(pw[:, 1, 0, :], rho[:].rearrange(f2), c1[:].rearrange(f2), ALU.mult, P_, P_)
                tt(pw[:, 1, 1, :], rho[:].rearrange(f2), s1[:].rearrange(f2), ALU.mult, P_, P_)
                t1f_, t2f_ = t1[:].rearrange(f2), t2[:].rearrange(f2)
                for k in range(2, ELL + 1):
                    tt(t1f_, pw[:, k - 1, 0, :], pw[:, 1, 0, :], ALU.mult, P_, P_)
                    tt(t2f_, pw[:, k - 1, 1, :], pw[:, 1, 1, :], ALU.mult, P_, P_)
                    tt(pw[:, k, 0, :], t1f_, t2f_, ALU.subtract, P_, P_)
                    tt(t1f_, pw[:, k - 1, 0, :], pw[:, 1, 1, :], ALU.mult, P_, P_)
                    tt(t2f_, pw[:, k - 1, 1, :], pw[:, 1, 0, :], ALU.mult, P_, P_)
                    tt(pw[:, k, 1, :], t1f_, t2f_, ALU.add, P_, P_)
                tt(t1[:], rho[:], c1[:], ALU.mult, P_, P_)
                ts(t1[:], t1[:], -1.0, None, ALU.add, None, P_, P_)
                tt(t2[:], rho[:], s1[:], ALU.mult, P_, P_)
                t3 = dtt
                tt(t3[:], lr[:], lr[:], ALU.mult, P_, P_)
                tt(qre[:], li[:], li[:], ALU.mult, P_, P_)
                tt(t3[:], t3[:], qre[:], ALU.add, P_, P_)
                S.op("dve", lambda: nc.vector.reciprocal(t3[:], t3[:]), reads=P_, writes=P_)
                tt(qre[:], t1[:], lr[:], ALU.mult, P_, P_)
                tt(qim[:], t2[:], li[:], ALU.mult, P_, P_)
                tt(qre[:], qre[:], qim[:], ALU.add, P_, P_)
                tt(qim[:], t2[:], lr[:], ALU.mult, P_, P_)
                tt(t2[:], t1[:], li[:], ALU.mult, P_, P_)
                tt(qim[:], qim[:], t2[:], ALU.subtract, P_, P_)
                tt(qre[:], qre[:], t3[:], ALU.mult, P_, P_)
                tt(qim[:], qim[:], t3[:], ALU.mult, P_, P_)
                tA4 = tA[:, :, 0:16].rearrange("p (kk c) h -> p kk c h", kk=4)
                tB4 = tB[:, :, 0:16].rearrange("p (kk c) h -> p kk c h", kk=4)
                tC4 = tA[:, :, 16:32].rearrange("p (kk c) h -> p kk c h", kk=4)
                tD4 = tB[:, :, 16:32].rearrange("p (kk c) h -> p kk c h", kk=4)
                bc4 = lambda ap32: ap32.rearrange("p (kk c) -> p kk c", kk=4).unsqueeze(3).to_broadcast([128, 4, 8, 16])
                qr_b, qi_b = bc4(qre[:].rearrange(f2)), bc4(qim[:].rearrange(f2))
                tt(tA4, bbr, qr_b, ALU.mult, [b_bb] + P_, [b_tA])
                tt(tB4, bbi, qi_b, ALU.mult, [b_bb] + P_, [b_tB])
                tt(tC4, bbi, qr_b, ALU.mult, [b_bb] + P_, [b_tA])
                tt(tD4, bbr, qi_b, ALU.mult, [b_bb] + P_, [b_tB])
                tt(bbr, tA4, tB4, ALU.subtract, [b_tA, b_tB], [b_bb])
                tt(bbi, tC4, tD4, ALU.add, [b_tA, b_tB], [b_bb])
                S.op("dve", lambda: nc.vector.memset(ZBre, 0.0), writes=[b_ZBre])
                S.op("dve", lambda: nc.vector.memset(ZBim, 0.0), writes=[b_ZBim])
                for i in range(ELL):
                    k = ELL - 1 - i
                    pr_b, pi_b = bc4(pw[:, k, 0, :]), bc4(pw[:, k, 1, :])
                    for ri in range(2):
                        if ri == 0:
                            tt(tA4, bbr, pr_b, ALU.mult, [b_bb] + P_, [b_tA])
                            tt(tB4, bbi, pi_b, ALU.mult, [b_bb] + P_, [b_tB])
                            op, ZB_, bZB_ = ALU.subtract, ZBre, b_ZBre
                        else:
                            tt(tA4, bbi, pr_b, ALU.mult, [b_bb] + P_, [b_tA])
                            tt(tB4, bbr, pi_b, ALU.mult, [b_bb] + P_, [b_tB])
                            op, ZB_, bZB_ = ALU.add, ZBim, b_ZBim
                        for g2 in range(2):
                            zv = ZB_[64 * g2:64 * g2 + 64].rearrange("p c (kk g h) -> p kk c g h", kk=4, g=2)[:, :, :, g2, :]
                            tt(zv, tA4[64 * g2:64 * g2 + 64], tB4[64 * g2:64 * g2 + 64], op, [b_tA, b_tB], [bZB_])
                        for hh in range(2):
                            bk = nb()
                            for j_ in range(4):
                                transpose_to(bk, j_ * 128, ZB_[:, 4 * hh + j_, :], 128, 128, [bZB_], inc=(j_ == 3))
                            act(BBT(i, ri)[:, 4 * hh:4 * hh + 4, :], ps[:, bk, :].rearrange("p (j t) -> p j t", j=4),
                                AF.Copy, [psb[bk]], [b_BBT])
                ts(ZBim, ZBim, -1.0, None, ALU.mult, None, [b_ZBim], [b_ZBim])
                for ri, (E_, bE_) in enumerate([(Ere, b_Ere), (Eim, b_Eim)]):
                    for hh in range(2):
                        bk = nb()
                        for j_ in range(4):
                            transpose_to(bk, j_ * 128, E_[:, 4 * hh + j_, :], 128, 128, [bE_], inc=(j_ == 3))
                        act(CTf[ri][:, 4 * hh:4 * hh + 4, :], ps[:, bk, :].rearrange("p (j t) -> p j t", j=4),
                            AF.Copy, [psb[bk]], [b_CTf[ri]])
                cp(id32[:], ident[:, 0:32], [b_ident], [b_dD])
                for kk in range(1, 4):
                    tt(id32[:], id32[:], ident[:, 32 * kk:32 * kk + 32], ALU.add, [b_ident, b_dD], [b_dD])
                rT = c8(rhoTf)
                b_rT = b_rho
                e4 = lambda ap: ap.rearrange("p c (kk m) -> p c kk m", kk=4)
                for k in range(ELL + 1):
                    pk = lambda ri_: pw[:, k, ri_, :].rearrange("p (kk c) -> p c kk", kk=4).unsqueeze(3).to_broadcast([128, 8, 4, 32])
                    tt(e4(Ere), e4(CTf[0]), pk(0), ALU.mult, [b_CTf[0]] + P_, [b_Ere])
                    tt(e4(rT), e4(CTf[1]), pk(1), ALU.mult, [b_CTf[1]] + P_, [b_rT])
                    tt(Ere, Ere, rT, ALU.subtract, [b_Ere, b_rT], [b_Ere])
                    tt(e4(Eim), e4(CTf[0]), pk(1), ALU.mult, [b_CTf[0]] + P_, [b_Eim])
                    tt(e4(rT), e4(CTf[1]), pk(0), ALU.mult, [b_CTf[1]] + P_, [b_rT])
                    tt(Eim, Eim, rT, ALU.add, [b_Eim, b_rT], [b_Eim])
                    if k >= 1:
                        act(CT[:, 2 * (k - 1), :, :], Ere, AF.Copy, [b_Ere], [b_CT])
                        ts(CT[:, 2 * (k - 1) + 1, :, :], Eim, -1.0, None, ALU.mult, None, [b_Eim], [b_CT])
                    if k <= ELL - 1:
                        bk = nb(2)
                        pk2 = ps[:, bk:bk + 2, :].rearrange("p b f -> p (b f)").rearrange("p (c m) -> p c m", c=8)
                        for c in range(8):
                            bkc = bk + c // 4
                            S.op("pe", lambda: nc.tensor.matmul(pk2[:, c, :], lhsT=ZBre[:, c, :], rhs=Ere[:, c, :],
                                                                start=True, stop=False),
                                 reads=[b_ZBre, b_Ere], writes=[psb[bkc]], inc=False)
                            S.op("pe", lambda: nc.tensor.matmul(pk2[:, c, :], lhsT=ZBim[:, c, :], rhs=Eim[:, c, :],
                                                                start=False, stop=True),
                                 reads=[b_ZBim, b_Eim], writes=[psb[bkc]], inc=(c % 4 == 3))
                        for kk in range(4):
                            blk = pk2[32 * kk:32 * kk + 32, :, 32 * kk:32 * kk + 32]
                            if k == 0:
                                tt(KT[32 * kk:32 * kk + 32, k, :, :],
                                   id32[32 * kk:32 * kk + 32, :].unsqueeze(1).to_broadcast([32, 8, 32]),
                                   dcol[32 * kk:32 * kk + 32, :].unsqueeze(2).to_broadcast([32, 8, 32]), ALU.mult,
                                   [b_dD], [b_KT])
                                tt(KT[32 * kk:32 * kk + 32, k, :, :], KT[32 * kk:32 * kk + 32, k, :, :], blk, ALU.add,
                                   [psb[bk], psb[bk + 1], b_KT], [b_KT])
                            else:
                                cp(KT[32 * kk:32 * kk + 32, k, :, :], blk, [psb[bk], psb[bk + 1]], [b_KT])
                rho4 = qre
                tt(rho4[:], rho[:], rho[:], ALU.mult, P_, P_)
                tt(rho4[:], rho4[:], rho4[:], ALU.mult, P_, P_)
                for _ in range(2):
                    tt(t1[:], c1[:], c1[:], ALU.mult, P_, P_)
                    tt(t2[:], s1[:], s1[:], ALU.mult, P_, P_)
                    tt(s1[:], c1[:], s1[:], ALU.mult, P_, P_)
                    ts(s1[:], s1[:], 2.0, None, ALU.mult, None, P_, P_)
                    tt(c1[:], t1[:], t2[:], ALU.subtract, P_, P_)
                cp(cosT[:, :, 0:1], c1[:].rearrange(f2).unsqueeze(2), P_ + [b_KT, b_CT], [b_cos])
                cp(sinT[:, :, 0:1], s1[:].rearrange(f2).unsqueeze(2), P_ + [b_KT, b_CT], [b_sin])
                m = 1
                while m < L:
                    pr = cosT[:, :, m - 1:m].to_broadcast([128, 32, m])
                    pi_ = sinT[:, :, m - 1:m].to_broadcast([128, 32, m])
                    tt(tA[:, :, :m], cosT[:, :, 0:m], pr, ALU.mult, TB, [b_tA])
                    tt(tB[:, :, :m], sinT[:, :, 0:m], pi_, ALU.mult, TB, [b_tB])
                    tt(cosT[:, :, m:2 * m], tA[:, :, :m], tB[:, :, :m], ALU.subtract, [b_tA, b_tB], [b_cos])
                    tt(tA[:, :, :m], cosT[:, :, 0:m], pi_, ALU.mult, TB, [b_tA])
                    tt(tB[:, :, :m], sinT[:, :, 0:m], pr, ALU.mult, TB, [b_tB])
                    tt(sinT[:, :, m:2 * m], tA[:, :, :m], tB[:, :, :m], ALU.add, [b_tA, b_tB], [b_sin])
                    m *= 2
                rho4f = rho4[:].rearrange(f2)
                cp(rhoT[:], rho4f.unsqueeze(2).to_broadcast([128, 32, L]), P_ + [b_rT], [b_rho])
                S.op("dve", lambda: nc.vector.memset(rhoT[:, :, 0:1], 0.0), reads=[b_rho], writes=[b_rho])
                for ri in range(2):
                    S.op("dve", lambda: nc.vector.memset(Xu[ri][:], 0.0), writes=[b_Xc])
                    S.op("dve", lambda: nc.vector.memset(Xs[ri][:], 0.0), writes=[b_Xc])

                Wi, Wga, Wgb = W[:, wb0 % 4], W[:, (wb0 + 1) % 4], W[:, (wb0 + 2) % 4]
                bWi, bWga, bWgb = bW[wb0 % 4], bW[(wb0 + 1) % 4], bW[(wb0 + 2) % 4]
                V4 = lambda ap: ap.rearrange("p (kk c) t -> p kk (c t)", kk=4)
                K4 = lambda ap: ap.rearrange("p (kk c) t -> p kk c t", kk=4)
                def load_X0():
                    for ri, nm in enumerate(["st_re", "st_im"]):
                        bk = nb()
                        for q4 in range(4):
                            S.dma("sp", sin_t[:, :], I[nm][:, 1024 * q4:1024 * q4 + 1024], writes=[b_sin_t])
                            for k8 in range(8):
                                k = q4 * 8 + k8
                                transpose_to(bk, k * 16, sin_t[:16, 128 * k8:128 * k8 + 128], 16, 128, [b_sin_t], inc=(k8 == 7))
                        cp(X0[ri].rearrange("p (kk c) b -> p kk c b", kk=4),
                           ps[:, bk, :].rearrange("p (c kk b) -> p kk c b", kk=4, b=16), [psb[bk]], [b_X0L[ri]])

                def stA(it):
                    t0, n = TS5[it]
                    smp = (it == 16)
                    tglob = 4 if smp else it // 4
                    nsb = n // ELL
                    par = it % 2
                    ubf, b_ubf = ubfL[par], b_ubfL[par]
                    spv, b_sp = spL[par], b_spL[par]
                    xrb4 = spv[:, :, 0:128].rearrange("p c (kk t) -> p kk c t", kk=4)
                    xib4 = spv[:, :, 128:256].rearrange("p c (kk t) -> p kk c t", kk=4)
                    ub3 = lambda kk_, c_, i_: ubf[32 * kk_:32 * kk_ + 32, c_, i_ * nsb:(i_ + 1) * nsb]
                    for fc in range(8):
                        bk = nb()
                        mm_acc(ps[:, bk, :n], [(Wi[:, kc, fc * 128:(fc + 1) * 128], xnT[:, kc, :n]) for kc in range(8)],
                               [bWi, b_xnT], psb[bk])
                        act(ubf[:, fc, :n].rearrange("p (i s) -> p s i", i=ELL),
                            ps[:, bk, :n].rearrange("p (s i) -> p s i", i=ELL), AF.Copy, [psb[bk]], [b_ubf])
                    stA2(it, smp, nsb, n, ubf, b_ubf, b_sp, xrb4, xib4, ub3)

                def stN(it):
                    t0, n = TS5[it]
                    act(sq[:, :, :n], x[:, :, t0:t0 + n], AF.Square, [bxs[it]], [b_sq])
                    bk = nb()
                    mm_acc(ps[:, bk, :n], [(onesb[:, :], sq[:, c, :n]) for c in range(8)], [b_sq, b_onesb], psb[bk])
                    act(rs[:, :n], ps[:, bk, :n], AF.Sqrt, [psb[bk], b_cst], [b_rs], scale=1.0 / D, bias=cst[:, 0:1])
                    S.op("dve", lambda: nc.vector.reciprocal(rs2[:, :n], rs[:, :n]), reads=[b_rs], writes=[b_rs2])
                    for c in range(8):
                        stt(xnT[:, c, :n], x[:, c, t0:t0 + n], gmix[:, l, c:c + 1], rs2[:, :n], ALU.mult, ALU.mult,
                            [bxs[it], b_rs2, b_g], [b_xnT])

                def stA2(it, smp, nsb, n, ubf, b_ubf, b_sp, xrb4, xib4, ub3):
                    bk0 = nb()
                    while bk0 % 4 != 0:
                        bk0 = nb()
                    for _ in range(3):
                        nb()
                    for kk in range(4):
                        for ri in range(2):
                            for c in range(8):
                                for i in range(ELL):
                                    last = (ri == 1 and c == 7 and i == ELL - 1)
                                    S.op("pe", lambda: nc.tensor.matmul(
                                        ps[:, bk0 + kk, ri * 256 + c * L: ri * 256 + c * L + nsb],
                                        lhsT=BBT(i, ri)[32 * kk:32 * kk + 32, c, :], rhs=ub3(kk, c, i),
                                        start=(i == 0), stop=(i == ELL - 1), tile_position=(32 * kk, 0),
                                        skip_group_check=True),
                                        reads=[b_BBT, b_ubf], writes=[psb[bk0 + kk]], inc=last)
                    pb = [psb[bk0 + kk] for kk in range(4)]
                    pre = ps[:, bk0:bk0 + 4, 0:256]
                    pim = ps[:, bk0:bk0 + 4, 256:512]
                    if not smp:
                        tt(V4(wre[:]), pre, V4(cosT[:]), ALU.mult, pb + TB, [b_wre])
                        tt(V4(tA[:]), pim, V4(sinT[:]), ALU.mult, pb + TB, [b_tA])
                        tt(wre[:], wre[:], tA[:], ALU.add, [b_wre, b_tA], [b_wre])
                        tt(V4(wim[:]), pim, V4(cosT[:]), ALU.mult, pb + TB, [b_wim])
                        tt(V4(tA[:]), pre, V4(sinT[:]), ALU.mult, pb + TB, [b_tA])
                        tt(wim[:], wim[:], tA[:], ALU.subtract, [b_wim, b_tA], [b_wim])
                        if it > 0:
                            tt(wre[:, :, 0:1], wre[:, :, 0:1], Xs[0][:].unsqueeze(2), ALU.add, [b_wre, b_Xc], [b_wre])
                            tt(wim[:, :, 0:1], wim[:, :, 0:1], Xs[1][:].unsqueeze(2), ALU.add, [b_wim, b_Xc], [b_wim])
                        cp(xrb4[:, :, :, 0:1], Xu[0][:].rearrange("p (kk c) -> p kk c", kk=4).unsqueeze(3), [b_Xc], [b_sp])
                        cp(xib4[:, :, :, 0:1], Xu[1][:].rearrange("p (kk c) -> p kk c", kk=4).unsqueeze(3), [b_Xc], [b_sp])
                        for (w_, bw_) in ((wre, b_wre), (wim, b_wim)):
                            wf = flat(w_)
                            S.op("dve", lambda: nc.vector.tensor_tensor_scan(wf, flat(rhoT), wf, 0.0, ALU.mult, ALU.add),
                                 reads=[bw_] + TB, writes=[bw_])
                        tt(tA[:], wre[:], cosT[:], ALU.mult, [b_wre] + TB, [b_tA])
                        tt(tB[:], wim[:], sinT[:], ALU.mult, [b_wim] + TB, [b_tB])
                        tt(xrb4[:, :, :, 1:L], K4(tA[:])[:, :, :, 0:L - 1], K4(tB[:])[:, :, :, 0:L - 1], ALU.subtract,
                           [b_tA, b_tB], [b_sp])
                        tt(Xu[0][:].unsqueeze(2), tA[:, :, L - 1:L], tB[:, :, L - 1:L], ALU.subtract, [b_tA, b_tB], [b_Xc])
                        tt(tA[:], wre[:], sinT[:], ALU.mult, [b_wre] + TB, [b_tA])
                        tt(tB[:], wim[:], cosT[:], ALU.mult, [b_wim] + TB, [b_tB])
                        tt(xib4[:, :, :, 1:L], K4(tA[:])[:, :, :, 0:L - 1], K4(tB[:])[:, :, :, 0:L - 1], ALU.add,
                           [b_tA, b_tB], [b_sp])
                        tt(Xu[1][:].unsqueeze(2), tA[:, :, L - 1:L], tB[:, :, L - 1:L], ALU.add, [b_tA, b_tB], [b_Xc])
                        tt(Xs[0][:], Xu[0][:], rho4f, ALU.mult, [b_Xc] + P_, [b_Xc])
                        tt(Xs[1][:], Xu[1][:], rho4f, ALU.mult, [b_Xc] + P_, [b_Xc])
                    else:
                        load_X0()
                        cp(xrb4[:, :, :, 0:NSEQ], X0[0].rearrange("p (kk c) b -> p kk c b", kk=4), b_X0L, [b_sp])
                        cp(xib4[:, :, :, 0:NSEQ], X0[1].rearrange("p (kk c) b -> p kk c b", kk=4), b_X0L, [b_sp])
                        p4r = pw[:, ELL, 0, :].unsqueeze(2).to_broadcast([128, 32, NSEQ])
                        p4i = pw[:, ELL, 1, :].unsqueeze(2).to_broadcast([128, 32, NSEQ])
                        vre = ps[:, bk0:bk0 + 4, 0:256].rearrange("p kk (c t) -> p kk c t", c=8)[:, :, :, 0:NSEQ]
                        vim = ps[:, bk0:bk0 + 4, 256:512].rearrange("p kk (c t) -> p kk c t", c=8)[:, :, :, 0:NSEQ]
                        tt(tA[:, :, 0:NSEQ], X0[0], p4r, ALU.mult, b_X0L + P_, [b_tA])
                        tt(tB[:, :, 0:NSEQ], X0[1], p4i, ALU.mult, b_X0L + P_, [b_tB])
                        tt(tA[:, :, 0:NSEQ], tA[:, :, 0:NSEQ], tB[:, :, 0:NSEQ], ALU.subtract, [b_tA, b_tB], [b_tA])
                        tt(K4(wre[:])[:, :, :, 0:NSEQ], vre, K4(tA[:])[:, :, :, 0:NSEQ], ALU.add, pb + [b_tA], [b_wre])
                        tt(tA[:, :, 0:NSEQ], X0[0], p4i, ALU.mult, b_X0L + P_, [b_tA])
                        tt(tB[:, :, 0:NSEQ], X0[1], p4r, ALU.mult, b_X0L + P_, [b_tB])
                        tt(tA[:, :, 0:NSEQ], tA[:, :, 0:NSEQ], tB[:, :, 0:NSEQ], ALU.add, [b_tA, b_tB], [b_tA])
                        tt(K4(wim[:])[:, :, :, 0:NSEQ], vim, K4(tA[:])[:, :, :, 0:NSEQ], ALU.add, pb + [b_tA], [b_wim])

                def stB(it):
                    t0, n = TS5[it]
                    smp = (it == 16)
                    tglob = 4 if smp else it // 4
                    nsb = n // ELL
                    par = it % 2
                    ubf, b_ubf = ubfL[par], b_ubfL[par]
                    spv, b_sp = spL[par], b_spL[par]
                    xrb4 = spv[:, :, 0:128].rearrange("p c (kk t) -> p kk c t", kk=4)
                    xib4 = spv[:, :, 128:256].rearrange("p c (kk t) -> p kk c t", kk=4)
                    ub3 = lambda kk_, c_, i_: ubf[32 * kk_:32 * kk_ + 32, c_, i_ * nsb:(i_ + 1) * nsb]
                    started = set()
                    bkY = [nb(), nb()]
                    for half in range(2):
                        bk = bkY[half]
                        js = [2 * half, 2 * half + 1]
                        for j_ in js:
                            col0 = (j_ % 2) * 256
                            for c in range(8):
                                for kk in range(4):
                                    outp = ps[32 * kk:32 * kk + 32, bk, col0 + c * L: col0 + c * L + nsb]
                                    first = (bk, kk) not in started
                                    started.add((bk, kk))
                                    S.op("pe", lambda: nc.tensor.matmul(
                                        outp, lhsT=CT[:, 2 * j_, c, 32 * kk:32 * kk + 32],
                                        rhs=spv[:, c, kk * 32:kk * 32 + nsb],
                                        start=first, stop=False, tile_position=(0, 32 * kk), skip_group_check=True),
                                        reads=[b_CT, b_sp], writes=[psb[bk]], inc=False)
                                    S.op("pe", lambda: nc.tensor.matmul(
                                        outp, lhsT=CT[:, 2 * j_ + 1, c, 32 * kk:32 * kk + 32],
                                        rhs=spv[:, c, 128 + kk * 32:128 + kk * 32 + nsb],
                                        start=False, stop=False, tile_position=(0, 32 * kk), skip_group_check=True),
                                        reads=[b_CT, b_sp], writes=[psb[bk]], inc=False)
                        for j_ in js:
                            col0 = (j_ % 2) * 256
                            for c in range(8):
                                for kk in range(4):
                                    outp = ps[32 * kk:32 * kk + 32, bk, col0 + c * L: col0 + c * L + nsb]
                                    for i in range(j_ + 1):
                                        fin = (j_ % 2 == 1 and c == 7 and kk == 3 and i == j_)
                                        S.op("pe", lambda: nc.tensor.matmul(
                                            outp, lhsT=KT[32 * kk:32 * kk + 32, j_ - i, c, :], rhs=ub3(kk, c, i),
                                            start=False, stop=True, tile_position=(32 * kk, 32 * kk), skip_group_check=True),
                                            reads=[b_KT, b_ubf], writes=[psb[bk]], inc=fin)
                    for j_ in range(ELL):
                        bk = bkY[j_ // 2]
                        col0 = (j_ % 2) * 256
                        act(gb[:, :, :n].rearrange("p c (s i) -> p c s i", i=ELL)[:, :, :, j_],
                            ps[:, bk, col0:col0 + 256].rearrange("p (c s) -> p c s", c=8)[:, :, :nsb], AF.Gelu,
                            [psb[bk]], [b_gb])
                    for oc in range(8):
                        bka = nb()
                        mm_acc(ps[:, bka, :n], [(Wga[:, fc, oc * 128:(oc + 1) * 128], gb[:, fc, :n]) for fc in range(8)],
                               [bWga, b_gb], psb[bka])
                        bkb = nb()
                        mm_acc(ps[:, bkb, :n], [(Wgb[:, fc, oc * 128:(oc + 1) * 128], gb[:, fc, :n]) for fc in range(8)],
                               [bWgb, b_gb], psb[bkb])
                        act(glu_a[:, :n], ps[:, bkb, :n], AF.Sigmoid, [psb[bkb]], [b_glu])
                        tt(glu_a[:, :n], ps[:, bka, :n], glu_a[:, :n], ALU.mult, [psb[bka], b_glu], [b_glu])
                        tt(x[:, oc, t0:t0 + n], x[:, oc, t0:t0 + n], glu_a[:, :n], ALU.add, [bxs[it], b_glu], [bxs[it]])

                def s5_state_outputs():
                    pvo = "(c kk q) -> q kk c"
                    with nc.allow_non_contiguous_dma(reason="state out"):
                        for kk in range(4):
                            S.dma("sp", O["sre_p"].rearrange(pvo, kk=4, q=128)[:, kk, :], Xu[0][:, 8 * kk:8 * kk + 8], reads=[b_Xc])
                            S.dma("sp", O["sim_p"].rearrange(pvo, kk=4, q=128)[:, kk, :], Xu[1][:, 8 * kk:8 * kk + 8], reads=[b_Xc])
                    for ri, (dst, src_t, bsrc) in enumerate([(O["sre_s"], wre, b_wre), (O["sim_s"], wim, b_wim)]):
                        for q4 in range(8):
                            bk = nb()
                            for j_ in range(4):
                                k = q4 * 4 + j_
                                c_, kk_ = k // 4, k % 4
                                transpose_to(bk, j_ * 128, src_t[:, kk_ * 8 + c_, 0:NSEQ], 128, 16, [bsrc], inc=(j_ == 3))
                            cp(sin_t[:16, 512 * (q4 % 2):512 * (q4 % 2) + 512], ps[:16, bk, :], [psb[bk]], [b_sin_t])
                            if q4 % 2 == 1:
                                S.dma("sp", dst[:, 1024 * (q4 // 2):1024 * (q4 // 2) + 1024], sin_t[:16, :], reads=[b_sin_t])

                NT5 = len(TS5)
                stN(0)
                stA(0)
                stN(1)
                for it in range(NT5):
                    if it + 1 < NT5:
                        stA(it + 1)
                        if it + 1 == NT5 - 1:
                            wissue(wb0 + 4)
                            s5_state_outputs()
                    if it + 2 < NT5:
                        stN(it + 2)
                    stB(it)
                wissue(wb0 + 6)
                S.barrier()

        for l in range(depth):
            kind = l % 3
            j = l // 3
            wb0 = 11 * l
            if kind == 0:
                sgu(l, j, wb0)
            elif kind == 1:
                rglru(l, j, wb0)
            else:
                s5(l, j, wb0)
            ffn(l, wb0 + 3)

        with ExitStack() as st:
            xf = [T(st, "xf%d" % i, [128, 8, 512], F32) for i in range(2)]; b_xf = S.bufs(2)
            yo = [T(st, "yo%d" % i, [128, 1024], F32) for i in range(4)]; b_yo = S.bufs(4)
            oc_ = [0]
            norm(0, gfin, xf[0], b_xf[0])
            for t in range(5):
                t0, n = TILES[t]
                if t + 1 < 5:
                    norm(t + 1, gfin, xf[(t + 1) % 2], b_xf[(t + 1) % 2])
                nblk = max(1, n // 128)
                rows = min(n, 128)
                for bi in range(nblk):
                    k_ = oc_[0] % 4
                    oc_[0] += 1
                    for hh in range(2):
                        bk = nb()
                        for j_ in range(4):
                            c = 4 * hh + j_
                            transpose_to(bk, j_ * 128, xf[t % 2][:, c, bi * 128:bi * 128 + rows], 128, rows,
                                         [b_xf[t % 2]], inc=(j_ == 3))
                        if hh == 0:
                            act(yo[k_][:rows, 0:512], ps[:rows, bk, :], AF.Copy, [psb[bk]], [b_yo[k_]])
                        else:
                            cp(yo[k_][:rows, 512:1024], ps[:rows, bk, :], [psb[bk]], [b_yo[k_]])
                    dst = O["y_p"][t0 + bi * 128:t0 + bi * 128 + rows, :] if t < 4 else O["y_s"][:, :]
                    S.dma("sp", dst, yo[k_][:rows, :], reads=[b_yo[k_]])
        S.finish("sp")
    return nc


_CACHE = {}


def kernel(**inputs):
    depth = int(os.environ.get("KDEPTH", "4"))
    if depth not in _CACHE:
        _CACHE[depth] = build(depth)
    nc = _CACHE[depth]
    f = lambda a: np.ascontiguousarray(np.asarray(a, dtype=np.float32))
    shared = {}
    for nm in ["norm_mix", "norm_ffn", "norm_f", "w_ff1", "w_ff2", "w_in_a", "sgu_g", "w_s", "b_s", "w_out_a",
               "w_in_b", "conv_w", "conv_b", "w_a", "b_a", "w_x", "b_x", "lam", "w_out_b", "w_in_c",
               "lam_re", "lam_im", "log_dt", "b_re", "b_im", "c_re", "c_im", "d_skip", "w_glu"]:
        if depth == 0 and nm.startswith("w_"):
            continue
        shared[nm] = f(inputs[nm])
    xp = f(inputs["x_prompt"]); xs = f(inputs["x_sample"])
    stc = f(inputs["state_rglru_conv"]); sth = f(inputs["state_rglru_h"])
    sre = f(inputs["state_s5_re"]); sim = f(inputs["state_s5_im"])
    in_maps = []
    for i in range(NCORES):
        sl = slice(i * NSEQ, (i + 1) * NSEQ)
        m = dict(shared)
        m["xp"] = xp[i]
        m["xs"] = xs[sl].reshape(NS, D)
        m["st_conv"] = stc[0, sl].reshape(NSEQ * 3, D)
        m["st_h"] = sth[0, sl].reshape(NSEQ, D)
        m["st_re"] = sre[0, sl].reshape(NSEQ, 4096)
        m["st_im"] = sim[0, sl].reshape(NSEQ, 4096)
        in_maps.append(m)
    res = run_bass_kernel_spmd(nc, in_maps, core_ids=list(range(NCORES)))
    R = res.results
    cat = lambda k, shp: np.stack([np.asarray(r[k], dtype=np.float32).reshape(shp) for r in R])
    y_p = cat("y_p", (NP, D))
    y_s = cat("y_s", (NSEQ, 4, D)).reshape(128, 4, D)
    v_s = np.concatenate([np.asarray(r["v_s"], dtype=np.float32).reshape(2, NSEQ, 4, D) for r in R], axis=1)
    conv_p = cat("conv_p", (3, D))[None]
    h_p = cat("h_p", (D,))[None]
    conv_s = cat("conv_s", (NSEQ, 3, D)).reshape(1, 128, 3, D)
    h_s = cat("h_s", (NSEQ, D)).reshape(1, 128, D)
    sre_p = cat("sre_p", (64, 64))[None]
    sim_p = cat("sim_p", (64, 64))[None]
    sre_s = cat("sre_s", (NSEQ, 64, 64)).reshape(1, 128, 64, 64)
    sim_s = cat("sim_s", (NSEQ, 64, 64)).reshape(1, 128, 64, 64)
    return (y_p, y_s, v_s, conv_p, h_p, conv_s, h_s, sre_p, sim_p, sre_s, sim_s)
```
